# Optimizing a Trainium2 kernel written in Bass

```python
import math
import jax, jax.numpy as jnp
from jax import lax
import numpy as np

D_MODEL = 1024
BATCH = 2
SEQ = 16384
DEPTH = 4

GRID_W = 64
CTX_LEN = 256
D_MIX = D_MODEL
POOL_WINDOWS = (2, 4, 8, 16)
D_POOL = D_MIX // 4
POOL_GROUP = D_POOL // len(POOL_WINDOWS)
SWA_HEADS = 4
SWA_KV_HEADS = 2
SWA_HEAD_DIM = 64
SWA_WINDOW = 128
BLOCK = 128
MLA_HEADS = 4
MLA_NOPE = 64
MLA_ROPE = 32
MLA_V = 64
MLA_Q_RANK = 256
MLA_KV_RANK = 128
SGU_HEADS = 4
SGU_WIDTH = D_MIX // 4
SGU_HEAD_DIM = SGU_WIDTH // SGU_HEADS
SGU_CHUNK = 128
D_FF = 4 * D_MODEL
ROPE_BASE = 10000.0
DN_ALPHA = (2 * DEPTH) ** 0.25
DN_BETA = (8 * DEPTH) ** -0.25
N_MOD = 6
EPS = 1e-6

W_POOL = D_POOL
W_SWA_Q = SWA_HEADS * SWA_HEAD_DIM
W_MLA_CQ = MLA_Q_RANK
W_SGU = 2 * SGU_WIDTH
W_SWA_KV = SWA_KV_HEADS * SWA_HEAD_DIM
W_MLA_CKV = MLA_KV_RANK
W_MLA_KR = MLA_ROPE
C_POOL = 0
C_SWA_Q = C_POOL + W_POOL
C_MLA_CQ = C_SWA_Q + W_SWA_Q
C_SGU = C_MLA_CQ + W_MLA_CQ
C_KV = C_SGU + W_SGU
C_SWA_K = C_KV
C_SWA_V = C_SWA_K + W_SWA_KV
C_MLA_CKV = C_SWA_V + W_SWA_KV
C_MLA_KR = C_MLA_CKV + W_MLA_CKV
D_IN = C_MLA_KR + W_MLA_KR

kernel_name = 'hybrid_pool_swa_mla_sgu_dit_trunk'

F32 = jnp.float32


def cols(z, start, width):
    return z[..., start:start + width]


def layer_norm(x):
    xf = x.astype(F32)
    mu = jnp.mean(xf, axis=-1, keepdims=True)
    var = jnp.mean(jnp.square(xf - mu), axis=-1, keepdims=True)
    return ((xf - mu) * lax.rsqrt(var + EPS)).astype(x.dtype)


def layer_norm_affine(x, g, b):
    return layer_norm(x) * g + b


def rms_norm(x, g):
    xf = x.astype(F32)
    y = xf * lax.rsqrt(jnp.mean(jnp.square(xf), axis=-1, keepdims=True) + EPS)
    return y.astype(x.dtype) * g


def modulate(x, shift, scale):
    return layer_norm(x) * (1 + scale) + shift


def axial_angles(n, d_rot):
    rows = n // GRID_W
    row = jnp.repeat(jnp.arange(rows), GRID_W).astype(F32)
    col = jnp.tile(jnp.arange(GRID_W), rows).astype(F32)
    d_ax = d_rot // 2
    inv = ROPE_BASE ** (-jnp.arange(0, d_ax, 2, dtype=F32) / d_ax)
    return row[:, None] * inv, col[:, None] * inv


def _rotate(x, ang):
    xf = x.astype(F32)
    x1, x2 = jnp.split(xf, 2, axis=-1)
    cos, sin = jnp.cos(ang), jnp.sin(ang)
    return jnp.concatenate([x1 * cos - x2 * sin, x2 * cos + x1 * sin], axis=-1).astype(x.dtype)


def apply_axial_rope(x, angles):
    xr, xc = jnp.split(x, 2, axis=-1)
    return jnp.concatenate([_rotate(xr, angles[0]), _rotate(xc, angles[1])], axis=-1)


def pool_mixer(za, pool_w, pool_scale):
    b, n, _ = za.shape
    g_n = len(POOL_WINDOWS)
    xg = za.reshape(b, n, g_n, POOL_GROUP)
    xf = xg.astype(F32)
    cs = jnp.concatenate([jnp.zeros((b, 1, g_n, POOL_GROUP), F32), jnp.cumsum(xf, axis=1)], axis=1)
    t = jnp.arange(n)
    means = []
    for gi, w in enumerate(POOL_WINDOWS):
        lo = jnp.clip(t - w // 2, 0, n)
        hi = jnp.clip(t + w // 2, 0, n)
        s = cs[:, hi, gi] - cs[:, lo, gi]
        means.append(s / (hi - lo).astype(F32)[None, :, None])
    d = (jnp.stack(means, axis=2) - xf).astype(za.dtype)
    y = jnp.einsum('bngc,gce->bnge', d, pool_w).reshape(b, n, D_POOL)
    return y * pool_scale


def swa_latent(q, k, v, k_c, v_c, sink):
    b, n, _, d = q.shape
    nb = n // BLOCK
    grp = SWA_HEADS // SWA_KV_HEADS
    scale = d ** -0.5
    qb = q.reshape(b, nb, BLOCK, SWA_KV_HEADS, grp, d)
    pad = ((0, 0), (BLOCK, BLOCK), (0, 0), (0, 0))
    kp = jnp.pad(k, pad).reshape(b, nb + 2, BLOCK, SWA_KV_HEADS, d)
    vp = jnp.pad(v, pad).reshape(b, nb + 2, BLOCK, SWA_KV_HEADS, d)
    kband = jnp.concatenate([kp[:, :-2], kp[:, 1:-1], kp[:, 2:]], axis=2)
    vband = jnp.concatenate([vp[:, :-2], vp[:, 1:-1], vp[:, 2:]], axis=2)
    s_band = jnp.einsum('bnqhgd,bnkhd->bnhgqk', qb, kband, preferred_element_type=F32) * scale
    qpos = jnp.arange(nb)[:, None] * BLOCK + jnp.arange(BLOCK)[None, :]
    kpos = (jnp.arange(nb)[:, None] - 1) * BLOCK + jnp.arange(3 * BLOCK)[None, :]
    rel = kpos[:, None, :] - qpos[:, :, None]
    valid = (jnp.abs(rel) <= SWA_WINDOW) & (kpos[:, None, :] >= 0) & (kpos[:, None, :] < n)
    s_band = jnp.where(valid[None, :, None, None], s_band, -jnp.inf)
    s_ctx = jnp.einsum('bnqhgd,bchd->bnhgqc', qb, k_c, preferred_element_type=F32) * scale
    sink_col = jnp.broadcast_to(sink.astype(F32).reshape(SWA_KV_HEADS, grp)[:, :, None, None],
                                s_band.shape[:-1] + (1,))
    p = jax.nn.softmax(jnp.concatenate([s_band, s_ctx, sink_col], axis=-1), axis=-1)
    nk = 3 * BLOCK
    n_ctx = k_c.shape[1]
    o = (jnp.einsum('bnhgqk,bnkhd->bnqhgd', p[..., :nk].astype(v.dtype), vband)
         + jnp.einsum('bnhgqc,bchd->bnqhgd', p[..., nk:nk + n_ctx].astype(v.dtype), v_c))
    return o.reshape(b, n, SWA_HEADS * d)


def swa_context(q_c, k_c, v_c, sink):
    b, n, _, d = q_c.shape
    grp = SWA_HEADS // SWA_KV_HEADS
    qg = q_c.reshape(b, n, SWA_KV_HEADS, grp, d)
    s = jnp.einsum('bqhgd,bkhd->bhgqk', qg, k_c, preferred_element_type=F32) * d ** -0.5
    sink_col = jnp.broadcast_to(sink.astype(F32).reshape(SWA_KV_HEADS, grp)[None, :, :, None, None],
                                s.shape[:-1] + (1,))
    p = jax.nn.softmax(jnp.concatenate([s, sink_col], axis=-1), axis=-1)[..., :n]
    o = jnp.einsum('bhgqk,bkhd->bqhgd', p.astype(v_c.dtype), v_c)
    return o.reshape(b, n, SWA_HEADS * d)


def mla_expand_q(cq, g, w_uq):
    b, n, _ = cq.shape
    q = (rms_norm(cq, g) @ w_uq).reshape(b, n, MLA_HEADS, MLA_NOPE + MLA_ROPE)
    return q[..., :MLA_NOPE], q[..., MLA_NOPE:]


def mla_expand_kv(ckv, g, w_ukv):
    b, n, _ = ckv.shape
    kv = (rms_norm(ckv, g) @ w_ukv).reshape(b, n, MLA_HEADS, MLA_NOPE + MLA_V)
    return kv[..., :MLA_NOPE], kv[..., MLA_NOPE:]


def kv_sources(zkv, kv_norm, w_ukv):
    b, n, _ = zkv.shape
    k = cols(zkv, C_SWA_K - C_KV, W_SWA_KV).reshape(b, n, SWA_KV_HEADS, SWA_HEAD_DIM)
    v = cols(zkv, C_SWA_V - C_KV, W_SWA_KV).reshape(b, n, SWA_KV_HEADS, SWA_HEAD_DIM)
    kn, vm = mla_expand_kv(cols(zkv, C_MLA_CKV - C_KV, W_MLA_CKV), kv_norm, w_ukv)
    kr = cols(zkv, C_MLA_KR - C_KV, W_MLA_KR)
    return k, v, kn, kr, vm


def mla_latent(qn, qr, kn, kr, vm, kn_c, kr_c, vm_c):
    b, n, h, _ = qn.shape
    nb = n // BLOCK
    scale = (MLA_NOPE + MLA_ROPE) ** -0.5
    qn_b = qn.reshape(b, nb, BLOCK, h, MLA_NOPE).transpose(1, 0, 2, 3, 4)
    qr_b = qr.reshape(b, nb, BLOCK, h, MLA_ROPE).transpose(1, 0, 2, 3, 4)

    def one_block(args):
        qn_i, qr_i = args
        s_lat = (jnp.einsum('bqhd,bkhd->bhqk', qn_i, kn, preferred_element_type=F32)
                 + jnp.einsum('bqhr,bkr->bhqk', qr_i, kr, preferred_element_type=F32)) * scale
        s_ctx = (jnp.einsum('bqhd,bkhd->bhqk', qn_i, kn_c, preferred_element_type=F32)
                 + jnp.einsum('bqhr,bkr->bhqk', qr_i, kr_c, preferred_element_type=F32)) * scale
        p = jax.nn.softmax(jnp.concatenate([s_lat, s_ctx], axis=-1), axis=-1)
        return (jnp.einsum('bhqk,bkhd->bqhd', p[..., :n].astype(vm.dtype), vm)
                + jnp.einsum('bhqk,bkhd->bqhd', p[..., n:].astype(vm.dtype), vm_c))

    o = lax.map(one_block, (qn_b, qr_b))
    return o.transpose(1, 0, 2, 3, 4).reshape(b, n, h * MLA_V)


def mla_context(qn, qr, kn, kr, vm):
    b, n, h, _ = qn.shape
    scale = (MLA_NOPE + MLA_ROPE) ** -0.5
    s = (jnp.einsum('bqhd,bkhd->bhqk', qn, kn, preferred_element_type=F32)
         + jnp.einsum('bqhr,bkr->bhqk', qr, kr, preferred_element_type=F32)) * scale
    p = jax.nn.softmax(s, axis=-1).astype(vm.dtype)
    return jnp.einsum('bhqk,bkhd->bqhd', p, vm).reshape(b, n, h * MLA_V)


def sgu_mixer(zd, norm_g, norm_b, w_s, b_s):
    b, n, _ = zd.shape
    z = jax.nn.gelu(zd, approximate=False)
    u, v = jnp.split(z, 2, axis=-1)
    v = layer_norm_affine(v, norm_g, norm_b)
    nc = n // SGU_CHUNK
    vc = v.reshape(b, nc, SGU_CHUNK, SGU_HEADS, SGU_HEAD_DIM)
    mixed = jnp.einsum('hpq,bcqhd->bcphd', w_s, vc) + b_s.T[None, None, :, :, None]
    return u * mixed.reshape(b, n, SGU_WIDTH)


def channel_mlp(h, w1, w2):
    return jnp.square(jax.nn.relu(h @ w1)) @ w2


def setup_inputs(seed: int = 0) -> dict:
    key = jax.random.key(seed)
    ks = jax.random.split(key, 32)
    L, D = DEPTH, D_MODEL

    def nrm(i, shape, s):
        return jax.random.normal(ks[i], shape, F32) * s

    gate_offset = jnp.repeat(jnp.array([0., 0., 1., 0., 0., 1.], F32), D)
    return {
        'x': nrm(0, (BATCH, SEQ, D), 1.0),
        'c': nrm(1, (BATCH, D), 1.0),
        'ctx': nrm(2, (BATCH, CTX_LEN, D), 1.0),
        'c_ctx': nrm(3, (D,), 1.0),
        'w_ada': nrm(4, (L, D, N_MOD * D), 0.25 * D ** -0.5),
        'b_ada': nrm(5, (L, N_MOD * D), 0.02) + gate_offset,
        'w_in': nrm(6, (L, D, D_IN), D ** -0.5),
        'pool_w': nrm(7, (L, len(POOL_WINDOWS), POOL_GROUP, POOL_GROUP), POOL_GROUP ** -0.5),
        'pool_scale': 1.0 + nrm(8, (L, D_POOL), 0.1),
        'swa_sink': nrm(9, (L, SWA_HEADS), 0.5),
        'mla_q_norm': 1.0 + nrm(10, (L, MLA_Q_RANK), 0.02),
        'mla_w_uq': nrm(11, (L, MLA_Q_RANK, MLA_HEADS * (MLA_NOPE + MLA_ROPE)), MLA_Q_RANK ** -0.5),
        'mla_kv_norm': 1.0 + nrm(12, (L, MLA_KV_RANK), 0.02),
        'mla_w_ukv': nrm(13, (L, MLA_KV_RANK, MLA_HEADS * (MLA_NOPE + MLA_V)), MLA_KV_RANK ** -0.5),
        'sgu_norm_g': 1.0 + nrm(14, (L, SGU_WIDTH), 0.02),
        'sgu_norm_b': nrm(15, (L, SGU_WIDTH), 0.02),
        'sgu_w': nrm(16, (L, SGU_HEADS, SGU_CHUNK, SGU_CHUNK), SGU_CHUNK ** -0.5),
        'sgu_b': 1.0 + nrm(17, (L, SGU_HEADS, SGU_CHUNK), 0.02),
        'w_out': nrm(18, (L, D_MIX, D), D_MIX ** -0.5 * DN_BETA),
        'ln1_g': 1.0 + nrm(19, (L, D), 0.02),
        'ln1_b': nrm(20, (L, D), 0.02),
        'w_ff1': nrm(21, (L, D, D_FF), D ** -0.5),
        'w_ff2': nrm(22, (L, D_FF, D), D_FF ** -0.5 * DN_BETA),
        'ln2_g': 1.0 + nrm(23, (L, D), 0.02),
        'ln2_b': nrm(24, (L, D), 0.02),
    }


def reference(x, c, ctx, c_ctx, w_ada, b_ada, w_in, pool_w, pool_scale, swa_sink,
              mla_q_norm, mla_w_uq, mla_kv_norm, mla_w_ukv, sgu_norm_g, sgu_norm_b,
              sgu_w, sgu_b, w_out, ln1_g, ln1_b, w_ff1, w_ff2, ln2_g, ln2_b):
    b, n, _ = x.shape
    n_ctx = ctx.shape[1]
    swa_ang = axial_angles(n, SWA_HEAD_DIM)
    swa_ang_h = (swa_ang[0][:, None, :], swa_ang[1][:, None, :])
    mla_ang = axial_angles(n, MLA_ROPE)
    mla_ang_h = (mla_ang[0][:, None, :], mla_ang[1][:, None, :])
    s_c = jax.nn.silu(c)
    s_cc = jax.nn.silu(c_ctx)

    for l in range(DEPTH):
        last = l == DEPTH - 1
        m = jnp.split((s_c @ w_ada[l] + b_ada[l])[:, None, :], N_MOD, axis=-1)
        n_mod_c = 2 if last else N_MOD
        mc = jnp.split(s_cc @ w_ada[l][:, :n_mod_c * D_MODEL] + b_ada[l][:n_mod_c * D_MODEL], n_mod_c)

        hc = modulate(ctx, mc[0], mc[1])
        if last:
            zkv_c = hc @ w_in[l][:, C_KV:]
        else:
            zc = hc @ w_in[l]
            zkv_c = zc[..., C_KV:]
        k_c, v_c, kn_c, kr_c, vm_c = kv_sources(zkv_c, mla_kv_norm[l], mla_w_ukv[l])

        h = modulate(x, m[0], m[1])
        z = h @ w_in[l]
        k, v, kn, kr, vm = kv_sources(z[..., C_KV:], mla_kv_norm[l], mla_w_ukv[l])
        k = apply_axial_rope(k, swa_ang_h)
        kr = apply_axial_rope(kr, mla_ang)
        q = apply_axial_rope(cols(z, C_SWA_Q, W_SWA_Q).reshape(b, n, SWA_HEADS, SWA_HEAD_DIM), swa_ang_h)
        qn, qr = mla_expand_q(cols(z, C_MLA_CQ, W_MLA_CQ), mla_q_norm[l], mla_w_uq[l])
        qr = apply_axial_rope(qr, mla_ang_h)
        y = jnp.concatenate([
            pool_mixer(cols(z, C_POOL, W_POOL), pool_w[l], pool_scale[l]),
            swa_latent(q, k, v, k_c, v_c, swa_sink[l]),
            mla_latent(qn, qr, kn, kr, vm, kn_c, kr_c, vm_c),
            sgu_mixer(cols(z, C_SGU, W_SGU), sgu_norm_g[l], sgu_norm_b[l], sgu_w[l], sgu_b[l]),
        ], axis=-1) @ w_out[l]
        x = layer_norm_affine(DN_ALPHA * x + m[2] * y, ln1_g[l], ln1_b[l])
        f = channel_mlp(modulate(x, m[3], m[4]), w_ff1[l], w_ff2[l])
        x = layer_norm_affine(DN_ALPHA * x + m[5] * f, ln2_g[l], ln2_b[l])

        if not last:
            q_c = cols(zc, C_SWA_Q, W_SWA_Q).reshape(b, n_ctx, SWA_HEADS, SWA_HEAD_DIM)
            qn_c, qr_c = mla_expand_q(cols(zc, C_MLA_CQ, W_MLA_CQ), mla_q_norm[l], mla_w_uq[l])
            yc = jnp.concatenate([
                pool_mixer(cols(zc, C_POOL, W_POOL), pool_w[l], pool_scale[l]),
                swa_context(q_c, k_c, v_c, swa_sink[l]),
                mla_context(qn_c, qr_c, kn_c, kr_c, vm_c),
                sgu_mixer(cols(zc, C_SGU, W_SGU), sgu_norm_g[l], sgu_norm_b[l], sgu_w[l], sgu_b[l]),
            ], axis=-1) @ w_out[l]
            ctx = layer_norm_affine(DN_ALPHA * ctx + mc[2] * yc, ln1_g[l], ln1_b[l])
            fc = channel_mlp(modulate(ctx, mc[3], mc[4]), w_ff1[l], w_ff2[l])
            ctx = layer_norm_affine(DN_ALPHA * ctx + mc[5] * fc, ln2_g[l], ln2_b[l])
    return x
```

```python
import numpy as np
import ml_dtypes
from contextlib import ExitStack

import concourse.bass as bass
import concourse.mybir as mybir
from concourse.bass_utils import run_bass_kernel_spmd

F32 = mybir.dt.float32
BF16 = mybir.dt.bfloat16
U8 = mybir.dt.uint8
AF = mybir.ActivationFunctionType
ALU = mybir.AluOpType

D = 1024
L = 4
TOWN = 4096
NTL = 32
NTOK = 4352
NCTX = 256
EPS = 1e-6
ALPHA = float((2 * L) ** 0.25)
SWA_SCALE = 64 ** -0.5
MLA_SCALE = 96 ** -0.5

NIN = 2496
C_POOL, C_SQ, C_SQS, C_CQ, C_U, C_SK, C_SKS, C_CKV, C_KR, C_KRS, C_TV, C_TS = (
    0, 256, 512, 768, 1024, 1280, 1536, 1792, 1920, 2016, 2112, 2240)


ENGS = ("pe", "act", "dve", "pool", "sp")


class Buf:
    __slots__ = ("name", "w", "rs", "const", "excl")

    def __init__(self, name, const=False, excl=False):
        self.name = name
        self.w = None
        self.rs = []
        self.const = const
        self.excl = excl


class Op:
    __slots__ = ("eng", "fn", "deps", "sig", "dma_key", "val", "sem", "inc")

    def __init__(self, eng, fn, dma_key=None, inc=16):
        self.eng = eng
        self.fn = fn
        self.deps = []
        self.sig = False
        self.dma_key = dma_key
        self.val = None
        self.sem = None
        self.inc = inc


class Tracker:
    def __init__(self, nc):
        self.nc = nc
        self.q = {e: [] for e in ENGS}
        self.last_dma = {}
        self.pending = {e: [] for e in ENGS}

    def _dep(self, o, p):
        if p is None or p is o:
            return
        if p.eng == o.eng and p.dma_key is None and o.dma_key is None and o.eng == "pe":
            return
        o.deps.append(p)

    mute = False

    def op(self, eng, fn, reads=(), writes=(), dma_key=None, inc=16):
        if self.mute:
            return None
        o = Op(eng, fn, dma_key, inc)
        if self.pending[eng]:
            for p in self.pending[eng]:
                if p.eng != eng or p.dma_key is not None:
                    o.deps.append(p)
            self.pending[eng] = []
        for b in reads:
            self._dep(o, b.w)
            if b.excl:
                for r in b.rs:
                    if r.eng != eng:
                        self._dep(o, r)
        for b in writes:
            self._dep(o, b.w)
            for r in b.rs:
                self._dep(o, r)
        if dma_key is not None:
            self._dep(o, self.last_dma.get(dma_key))
            self.last_dma[dma_key] = o
        for b in reads:
            if not b.const:
                b.rs.append(o)
        for b in writes:
            b.w = o
            b.rs = []
        self.q[eng].append(o)
        return o

    def barrier(self):
        lasts = []
        for e in ENGS:
            for o in reversed(self.q[e]):
                if o.dma_key is None:
                    lasts.append(o)
                    break
        lasts += list(self.last_dma.values())
        for e in ENGS:
            self.pending[e] = list(lasts)

    def finalize(self, stack):
        nc = self.nc
        for e in ENGS:
            for o in self.q[e]:
                for p in o.deps:
                    p.sig = True
        self.esem = {e: stack.enter_context(nc.semaphore("es_" + e)) for e in ENGS}
        self.dsem = {}
        self.dcnt = {}
        for e in ENGS:
            cnt = 0
            for o in self.q[e]:
                if o.dma_key is not None:
                    k = o.dma_key
                    if k not in self.dsem:
                        self.dsem[k] = stack.enter_context(nc.semaphore("ds_%d" % len(self.dsem)))
                        self.dcnt[k] = 0
                    self.dcnt[k] += o.inc
                    o.sem = self.dsem[k]
                    o.val = self.dcnt[k]
                elif o.sig:
                    cnt += 1
                    o.sem = self.esem[e]
                    o.val = cnt

    def emit(self, eng, engobj):
        seen = {}
        for o in self.q[eng]:
            for p in o.deps:
                key = id(p.sem)
                if seen.get(key, 0) >= p.val:
                    continue
                seen[key] = p.val
                engobj.wait_ge(p.sem, p.val)
            ins = o.fn(engobj)
            if o.dma_key is not None:
                if o.inc == 16:
                    ins.then_inc(o.sem, 16)
                else:
                    ins.then_inc(o.sem)
            elif o.sig:
                ins.then_inc(o.sem, 1)

    def final_waits(self, engobj):
        for k, s in self.dsem.items():
            engobj.wait_ge(s, self.dcnt[k])


class TT:
    __slots__ = ("ap", "buf")

    def __init__(self, ap, buf):
        self.ap = ap
        self.buf = buf

    def __getitem__(self, idx):
        return TT(self.ap[idx], self.buf)

    def v(self, fn):
        return TT(fn(self.ap), self.buf)


class Builder:
    ARENA = 204 * 1024

    def __init__(self, layers, want_ctx_out):
        self.layers = list(layers)
        self.want_ctx_out = want_ctx_out
        self.nc = bass.Bass("TRN2", target_bir_lowering=False)
        self.T = Tracker(self.nc)
        self.dkeys = 0
        self.uid = 0
        self.keymap = {}

    def din(self, name, shape, dt=F32):
        t = self.nc.dram_tensor(name, list(shape), dt, kind="ExternalInput")
        return TT(t.ap(), Buf(name, const=True))

    def dout(self, name, shape, dt=F32):
        t = self.nc.dram_tensor(name, list(shape), dt, kind="ExternalOutput")
        return TT(t.ap(), Buf(name))

    def dscr(self, name, shape, dt):
        t = self.nc.dram_tensor(name, list(shape), dt)
        return TT(t.ap(), Buf(name))

    def alloc(self, shape, dt, name=None):
        nb = 4 if dt == F32 else 2
        n = int(np.prod(shape[1:])) * nb
        n_al = (n + 31) // 32 * 32
        off = self.aoff
        self.aoff += n_al
        assert self.aoff <= self.ARENA, ("arena overflow", name, self.aoff)
        self.amax = max(self.amax, self.aoff)
        ap = self.arena[:, off:off + n].bitcast(dt)
        if len(shape) == 3:
            ap = ap.rearrange("p (a b) -> p a b", a=shape[1])
        elif len(shape) == 4:
            ap = ap.rearrange("p (a b c) -> p a b c", a=shape[1], b=shape[2])
        if shape[0] < 128:
            ap = ap[0:shape[0]]
        self.uid += 1
        return TT(ap, Buf("%s_%d" % (name or "t", self.uid)))

    def slots(self, n, shape, dt, name=None):
        return [self.alloc(shape, dt, name) for _ in range(n)]

    def bank(self, i, shape, dt=F32):
        ap = self.banks[i][:, :].bitcast(dt)
        nel = 512 if dt == F32 else 1024
        tot = int(np.prod(shape[1:]))
        assert tot <= nel
        ap = ap[:, 0:tot]
        if len(shape) == 3:
            ap = ap.rearrange("p (a b) -> p a b", a=shape[1])
        if shape[0] < 128:
            ap = ap[0:shape[0]]
        return TT(ap, self.bbuf[i])

    def nb(self, pool):
        i = pool[0]
        pool.append(pool.pop(0))
        return i

    def dkey(self, p="k"):
        self.dkeys += 1
        return "%s%d" % (p, self.dkeys)

    def R(self, xs):
        return [x.buf for x in xs if x is not None]

    NKEYS = 90

    def pkey(self, key, eng="sp"):
        km = self.keymap.setdefault(eng, {})
        if key not in km:
            km[key] = "%s%d" % (eng, len(km) % (60 if eng == "sp" else 25))
        return km[key]

    def dma(self, eng, out, in_, key, extra_r=(), extra_w=()):
        key = self.pkey(key, eng)
        self.T.op(eng, lambda e, o=out.ap, i=in_.ap: e.dma_start(out=o, in_=i),
                  reads=[in_.buf] + self.R(extra_r), writes=[out.buf] + self.R(extra_w), dma_key=key)

    def dma_nc(self, eng, out, in_, key):
        nc = self.nc

        def fn(e, o=out.ap, i=in_.ap):
            with nc.allow_non_contiguous_dma(reason="small strided"):
                return e.dma_start(out=o, in_=i)
        self.T.op(eng, fn, reads=[in_.buf], writes=[out.buf], dma_key=self.pkey(key, eng))

    def mm(self, out, lhsT, rhs, start, stop, sg=False):
        self.T.op("pe", lambda e, o=out.ap, l=lhsT.ap, r=rhs.ap, s=start, p=stop, sg=sg:
                  e.matmul(o, lhsT=l, rhs=r, start=s, stop=p, skip_group_check=sg),
                  reads=[lhsT.buf, rhs.buf], writes=[out.buf])

    def tr(self, out, in_):
        self.T.op("pe", lambda e, o=out.ap, i=in_.ap, d=self.ident.ap: e.transpose(out=o, in_=i, identity=d),
                  reads=[in_.buf, self.ident.buf], writes=[out.buf])

    def act(self, out, in_, func, bias=None, scale=None):
        kw = {}
        rd = [in_.buf]
        if bias is not None:
            if isinstance(bias, TT):
                kw["bias"] = bias.ap
                rd.append(bias.buf)
            else:
                kw["bias"] = bias
        if scale is not None:
            if isinstance(scale, TT):
                kw["scale"] = scale.ap
                rd.append(scale.buf)
            else:
                kw["scale"] = scale
        self.T.op("act", lambda e, o=out.ap, i=in_.ap, f=func, kw=kw: e.activation(out=o, in_=i, func=f, **kw),
                  reads=rd, writes=[out.buf])

    def tt(self, eng, out, in0, in1, op):
        self.T.op(eng, lambda e, o=out.ap, a=in0.ap, b=in1.ap, op=op: e.tensor_tensor(out=o, in0=a, in1=b, op=op),
                  reads=[in0.buf, in1.buf], writes=[out.buf])

    def ts(self, eng, out, in0, s1, op0, s2=None, op1=None):
        rd = [in0.buf]
        a1 = s1
        a2 = s2
        if isinstance(s1, TT):
            a1 = s1.ap
            rd.append(s1.buf)
        if isinstance(s2, TT):
            a2 = s2.ap
            rd.append(s2.buf)
        if op1 is None:
            fn = lambda e, o=out.ap, a=in0.ap: e.tensor_scalar(out=o, in0=a, scalar1=a1, scalar2=None, op0=op0)
        else:
            fn = lambda e, o=out.ap, a=in0.ap: e.tensor_scalar(out=o, in0=a, scalar1=a1, scalar2=a2, op0=op0, op1=op1)
        self.T.op(eng, fn, reads=rd, writes=[out.buf])

    def stt(self, out, in0, scalar, in1, op0, op1):
        rd = [in0.buf, in1.buf]
        sc = scalar
        if isinstance(scalar, TT):
            sc = scalar.ap
            rd.append(scalar.buf)
        self.T.op("dve", lambda e, o=out.ap, a=in0.ap, b=in1.ap: e.scalar_tensor_tensor(
            out=o, in0=a, scalar=sc, in1=b, op0=op0, op1=op1), reads=rd, writes=[out.buf])

    def memset(self, eng, out, val):
        self.T.op(eng, lambda e, o=out.ap, v=val: e.memset(o, v), writes=[out.buf])

    def recip(self, out, in_):
        self.T.op("dve", lambda e, o=out.ap, i=in_.ap: e.reciprocal(out=o, in_=i), reads=[in_.buf], writes=[out.buf])

    def ln_stats(self, x, nfree, work):
        stats, mv, ve, rstd, nb = work
        nchunk = (nfree + 511) // 512
        for i in range(nchunk):
            lo = i * 512
            hi = min(nfree, lo + 512)
            self.T.op("dve", lambda e, o=stats.ap[:, i, :], a=x.ap[:, lo:hi]: e.bn_stats(out=o, in_=a),
                      reads=[x.buf], writes=[stats.buf])
        self.T.op("dve", lambda e, o=mv.ap, a=stats.ap[:, 0:nchunk, :].rearrange("p a b -> p (a b)"):
                  e.bn_aggr(out=o, in_=a), reads=[stats.buf], writes=[mv.buf])
        self.ts("dve", ve, mv[:, 1:2], EPS, ALU.add)
        self.act(ve, ve, AF.Ln)
        self.act(rstd, ve, AF.Exp, scale=-0.5)
        self.ts("dve", nb, mv[:, 0:1], -1.0, ALU.mult, rstd, ALU.mult)
        return rstd, nb

    def ln_work(self):
        return (self.alloc([128, 2, 6], F32, "stats"), self.alloc([128, 2], F32, "mv"),
                self.alloc([128, 1], F32, "ve"), self.alloc([128, 1], F32, "rstd"),
                self.alloc([128, 1], F32, "nb"))

    def phase(self):
        self.T.barrier()
        self.keymap = {}
        self.aoff = self.apersist

    def build(self):
        nc = self.nc
        with ExitStack() as st:
            self.st = st
            self.arena = st.enter_context(nc.sbuf_tensor("arena", [128, self.ARENA], U8))
            self.psum_all = st.enter_context(nc.psum_tensor("psum_all", [128, 16384], U8))
            self.banks = [self.psum_all[:, i * 2048:(i + 1) * 2048] for i in range(8)]
            self.bbuf = [Buf("bank%d" % i, excl=True) for i in range(8)]
            self.aoff = 0
            self.amax = 0
            self.declare()
            self.prologue()
            self.apersist = self.aoff
            for li, l in enumerate(self.layers):
                self.layer(li, l)
            self.T.barrier()
            self.T.finalize(st)
            T = self.T
            with nc.Block() as block:
                @block.sync
                def _(e):
                    T.emit("sp", e)

                @block.tensor
                def _(e):
                    T.emit("pe", e)

                @block.scalar
                def _(e):
                    T.emit("act", e)

                @block.vector
                def _(e):
                    T.emit("dve", e)

                @block.gpsimd
                def _(e):
                    T.emit("pool", e)
                    T.final_waits(e)
        return nc

    def declare(self):
        nl = len(self.layers)
        self.nl = nl
        d = self.din
        self.xin = d("xin", [NTOK, D])
        self.ccols = d("ccols", [128, 2, 8])
        SM = _DBG_SMALL
        self.w_ada = d("w_ada", [nl, D, 6 * D] if not SM else [nl, 128, 128])
        self.b_ada = d("b_ada", [nl, 6 * D])
        self.w_inx = d("w_inx", [nl, D, NIN])
        self.w_uqx = d("w_uqx", [nl, 256, 768])
        self.w_ukvx = d("w_ukvx", [nl, 128, 512])
        self.pool_bd = d("pool_bd", [nl, 256, 128])
        self.sgu_wT = d("sgu_wT", [nl, 512, 128])
        self.w_out = d("w_out", [nl, D, D] if not SM else [nl, 128, 128])
        self.w_ff1 = d("w_ff1", [nl, D, 4 * D] if not SM else [nl, 128, 128])
        self.w_ff2 = d("w_ff2", [nl, 4 * D, D] if not SM else [nl, 128, 128])
        self.colvec = d("colvec", [nl, 128, 8])
        self.bsT = d("bsT", [nl, 128, 256])
        self.rowvec = d("rowvec", [nl, 6, D])
        self.rope = d("rope", [4, 128, NTOK])
        self.rcnt = d("rcnt", [128, 2, NTOK])
        self.masks = d("masks", [128, 4, 128], BF16)
        self.sel = d("sel", [128, 8])
        self.ident_in = d("ident", [128, 128], BF16)
        self.y = self.dout("y", [TOWN, D])
        if self.want_ctx_out:
            self.yc = self.dout("yc", [NCTX, D])
        s = self.dscr
        self.wb_ada = s("wb_ada", [nl, D, 6 * D], BF16)
        self.wb_inx = s("wb_inx", [nl, D, NIN], BF16)
        self.wb_uqx = s("wb_uqx", [nl, 256, 768], BF16)
        self.wb_ukvx = s("wb_ukvx", [nl, 128, 512], BF16)
        self.wb_pool = s("wb_pool", [nl, 256, 128], BF16)
        self.wb_sgu = s("wb_sgu", [nl, 512, 128], BF16)
        self.wb_out = s("wb_out", [nl, D, D], BF16)
        self.wb_ff1 = s("wb_ff1", [nl, D, 4 * D], BF16)
        self.wb_ff2 = s("wb_ff2", [nl, 4 * D, D], BF16)
        self.xs = s("xs", [NTOK, D], F32)
        self.x1s = s("x1s", [NTOK, D], F32)
        self.mrow = s("mrow", [nl, 2, 6 * D], F32)
        self.zaT = s("zaT", [128, 2, 4384], F32)
        self.qsT = s("qsT", [128, 2, NTOK], BF16)
        self.ksT = s("ksT", [128, 2, 36 * 128], BF16)
        self.vs = s("vs", [36 * 128, 128], BF16)
        self.QmT = s("QmT", [96, 4, NTOK], BF16)
        self.KmT = [s("KmT%d" % j, [96, 4096], BF16) for j in range(4)]
        self.KmTc = s("KmTc", [96, 4, NCTX], BF16)
        self.Vm = [s("Vm%d" % j, [128, 4096], BF16) for j in range(4)]
        self.Vmc = s("Vmc", [NCTX, 512], BF16)
        self.ysT = s("ysT", [128, 2, NTOK], BF16)
        self.KmT_all = [s("KmT_all%d" % j, [4 * 96, 4096], BF16) for j in range(4)]
        self.Vm_all = [s("Vm_all%d" % j, [512, 4096], BF16) for j in range(4)]
        self.hk_in = s("hk_in", [128, 512], BF16)
        self.hk_all = s("hk_all", [512, 512], BF16)
        self.hv_in = s("hv_in", [256, 128], BF16)
        self.hv_all = s("hv_all", [1024, 128], BF16)
        self.hz_in = s("hz_in", [128, 32], F32)
        self.hz_all = s("hz_all", [512, 32], F32)

    def cast_layer_weights(self, li):
        ck = ["cast%d" % i for i in range(4)]
        n = [0]

        def cast(dst, src, rows, cols):
            for r0 in range(0, rows, 1024):
                r1 = min(rows, r0 + 1024)
                for c0 in range(0, cols, 1024):
                    c1 = min(cols, c0 + 1024)
                    self.dma("pool", TT(dst.ap[li, r0:r1, c0:c1], dst.buf), TT(src.ap[li, r0:r1, c0:c1], src.buf),
                             ck[n[0] % 4])
                    n[0] += 1
        if not _DBG_SMALL:
            cast(self.wb_ada, self.w_ada, D, 6 * D)
        cast(self.wb_inx, self.w_inx, D, NIN)
        cast(self.wb_uqx, self.w_uqx, 256, 768)
        cast(self.wb_ukvx, self.w_ukvx, 128, 512)
        cast(self.wb_pool, self.pool_bd, 256, 128)
        cast(self.wb_sgu, self.sgu_wT, 512, 128)
        if not _DBG_SMALL:
            cast(self.wb_out, self.w_out, D, D)
            cast(self.wb_ff1, self.w_ff1, D, 4 * D)
            cast(self.wb_ff2, self.w_ff2, 4 * D, D)

    def prologue(self):
        a = self.alloc
        self.cast_layer_weights(0)
        self.ident = a([128, 128], BF16, "ident")
        self.ident.buf.const = True
        self.dma("sp", self.ident, self.ident_in, "c_id")
        self.masks_sb = a([128, 4, 128], BF16, "masks")
        self.dma("sp", self.masks_sb, self.masks, "c_mk")
        self.sel_sb = a([128, 8], F32, "sel")
        self.dma("sp", self.sel_sb, self.sel, "c_sel")
        self.neghalf = a([128, 512], F32, "neghalf")
        self.memset("pool", self.neghalf, -0.5)
        self.ones_bf = a([128, 128], BF16, "ones_bf")
        self.memset("pool", self.ones_bf, 1.0)
        self.onesq = a([128, 128], BF16, "onesq")
        self.memset("pool", self.onesq, 1.0 / 256)
        self.oneskv = a([128, 128], BF16, "oneskv")
        self.memset("pool", self.oneskv, 1.0 / 128)
        self.ones_f = a([128, 128], F32, "ones_f")
        self.memset("pool", self.ones_f, 1.0)
        self.onesA = a([128, 128], BF16, "onesA")
        self.memset("pool", self.onesA, 0.0)
        self.memset("pool", self.onesA[:, 0:64], 1.0)
        self.onesB = a([128, 128], BF16, "onesB")
        self.memset("pool", self.onesB, 0.0)
        self.memset("pool", self.onesB[:, 64:128], 1.0)
        cc = a([128, 2, 8], F32, "cc")
        self.dma("sp", cc, self.ccols, "c_cc")
        self.s_bf = a([128, 8, 2], BF16, "s_bf")
        self.act(self.s_bf.v(lambda p: p.rearrange("p k s -> p s k")), cc, AF.Silu)
        self.mcol = a([128, 2, 48], F32, "mcol")
        zt = a([128, 2, 8], F32, "zt")
        self.memset("pool", zt, 0.0)
        self.dma("sp", self.zaT[:, :, 4112:4120], zt, "c_z1")
        self.dma("sp", self.zaT[:, :, 4376:4384], zt, "c_z2")

    def layer(self, li, l):
        last = (l == L - 1)
        xsrc = self.xin if li == 0 else self.xs
        stop = getattr(self, "stop_after", None)
        if not _DBG_SMALL:
            self.phase_M(li)
        if stop == "M":
            return
        self.phase_A(li, xsrc)
        if stop == "A":
            return
        self.phase_X(li)
        if stop == "X":
            return
        if li + 1 < self.nl:
            self.cast_layer_weights(li + 1)
        do_ctx = (not last)
        self.phase_B(li, xsrc, do_ctx)
        if stop == "B":
            return
        final = (li == self.nl - 1)
        self.phase_C(li, do_ctx, final)

    def phase_M(self, li):
        self.phase()
        a = self.alloc
        blk = self.slots(2, [128, 8, 1024], BF16, "adablk")
        bbc = a([2, 6 * D], F32, "bbc")
        self.dma("sp", bbc, self.b_ada.v(lambda p: p[li:li + 1, :].partition_broadcast(2)), "m_b")
        mr = a([2, 6 * D], F32, "mr")
        pool = [0, 1, 2, 3]
        for jb in range(6):
            w = blk[jb % 2]
            self.dma("sp", w, self.wb_ada.v(lambda p: p[li, :, jb * 1024:(jb + 1) * 1024].rearrange(
                "(k p) n -> p k n", p=128)), "m_w%d" % (jb % 2))
            for hf in range(2):
                ps = self.bank(self.nb(pool), [2, 512])
                for k in range(8):
                    self.mm(ps, self.s_bf[:, k, :], w[:, k, hf * 512:(hf + 1) * 512], k == 0, k == 7)
                c0 = jb * 1024 + hf * 512
                if jb in (1, 4):
                    self.stt(mr[:, c0:c0 + 512], ps, 1.0, bbc[:, c0:c0 + 512], ALU.add, ALU.add)
                else:
                    self.tt("dve", mr[:, c0:c0 + 512], ps, bbc[:, c0:c0 + 512], ALU.add)
        self.dma("pool", self.mrow.v(lambda p: p[li]), mr, "m_st")
        self.dma_nc("sp", self.mcol, self.mrow.v(lambda p: p[li].rearrange("s (j p) -> p s j", p=128)), "m_col")

    def mrow_bc(self, li, s, j):
        return self.mrow.v(lambda p: p[li, s:s + 1, j * D:(j + 1) * D].partition_broadcast(128))

    def ln_mod_T(self, xt, hT_dst, s, jscale, jshift, lnw, xn, tmp, tpool):
        rstd, nb = self.ln_stats(xt, D, lnw)
        self.act(xn, xt, AF.Identity, bias=nb, scale=rstd)
        pt = self.bank(self.nb(tpool), [128, 8, 128], BF16)
        for k in range(8):
            self.tr(pt[:, k, :], xn[:, k * 128:(k + 1) * 128])
        sc = self.mcol.v(lambda p: p[:, s, jscale * 8:(jscale + 1) * 8].unsqueeze(2).broadcast_to([128, 8, 128]))
        sh = self.mcol.v(lambda p: p[:, s, jshift * 8:(jshift + 1) * 8].unsqueeze(2).broadcast_to([128, 8, 128]))
        self.tt("dve", tmp, pt, sc, ALU.mult)
        self.tt("dve", hT_dst, tmp, sh, ALU.add)

    def groups(self, with_ctx=True, gsz=4, nlim=None):
        gs = []
        for t0 in range(0, NTL, gsz):
            gs.append((t0, gsz, 0))
        if nlim is not None:
            gs = gs[:nlim]
        if with_ctx:
            gs.append((NTL, 2, 1))
        return gs

    def phase_A(self, li, xsrc):
        self.phase()
        a = self.alloc
        winx = a([128, 8, NIN], BF16, "winx")
        for c in range(4):
            self.dma("sp", winx[:, 2 * c:2 * c + 2, :], self.wb_inx.v(
                lambda p: p[li, c * 256:(c + 1) * 256, :].rearrange("(k p) n -> p k n", p=128)), "a_w%d" % c)
        wuq = a([128, 2, 768], BF16, "wuq")
        self.dma("sp", wuq, self.wb_uqx.v(lambda p: p[li].rearrange("(k p) n -> p k n", p=128)), "a_wuq")
        wukv = a([128, 512], BF16, "wukv")
        self.dma("sp", wukv, self.wb_ukvx.v(lambda p: p[li]), "a_wukv")
        sguw = a([128, 4, 128], BF16, "sguw")
        self.dma("sp", sguw, self.wb_sgu.v(lambda p: p[li].rearrange("(h q) n -> q h n", q=128)), "a_sguw")
        bsT = a([128, 2, 128], F32, "bsT")
        self.dma("sp", bsT, self.bsT.v(lambda p: p[li].rearrange("p (a b) -> p a b", a=2)), "a_bsT")
        cv = a([128, 8], F32, "cv")
        self.dma("sp", cv, self.colvec.v(lambda p: p[li]), "a_cv")
        sgg = a([128, 256], F32, "sgg")
        sgb = a([128, 256], F32, "sgb")
        self.dma("sp", sgg, self.rowvec.v(lambda p: p[li, 0:1, 0:256].partition_broadcast(128)), "a_sgg")
        self.dma("sp", sgb, self.rowvec.v(lambda p: p[li, 0:1, 256:512].partition_broadcast(128)), "a_sgb")
        xts = self.slots(2, [128, D], F32, "xt")
        xns = self.slots(2, [128, D], BF16, "xn")
        tmps = self.slots(2, [128, 8, 128], F32, "tmp")
        hTs = self.slots(2, [128, 8, 512], BF16, "hT")
        ropes = self.slots(2, [128, 4, 512], F32, "rope")
        lnws = [self.ln_work() for _ in range(2)]
        lnw2 = [self.ln_work() for _ in range(2)]
        za_sb = self.slots(2, [128, 2, 512], F32, "za")
        t1s = self.slots(2, [128, 512], F32, "t1")
        t2s = self.slots(2, [128, 512], F32, "t2")
        qs_sb = self.slots(2, [128, 2, 512], BF16, "qs")
        ks_sb = self.slots(2, [128, 2, 512], BF16, "ks")
        sq = self.slots(1, [128, 2, 512], BF16, "sq")[0]
        cqg = self.slots(1, [128, 2, 512], BF16, "cqg")[0]
        veq = a([128, 512], F32, "veq")
        rstdq = a([128, 512], F32, "rstdq")
        rcq = a([128, 512], F32, "rcq")
        rsq = a([128, 512], F32, "rsq")
        Qm_sb = self.slots(2, [96, 4, 512], BF16, "Qm")
        uT = self.slots(2, [128, 2, 512], F32, "uT")
        sqk = a([128, 512], BF16, "sqk")
        ckvg = a([128, 512], BF16, "ckvg")
        vek = a([128, 512], F32, "vek")
        rstdk = a([128, 512], F32, "rstdk")
        Km_sb = self.slots(2, [96, 4, 512], BF16, "Km")
        vs_sb = self.slots(2, [128, 128], BF16, "vs")
        vg = self.slots(2, [128, 256], F32, "vg")
        vn = self.slots(2, [128, 256], F32, "vn")
        vpad = self.slots(2, [128, 4, 128], BF16, "vpad")
        for vp in vpad:
            self.memset("pool", vp, 0.0)
        tsg = self.slots(2, [128, 128], F32, "tsg")
        ys_sb = self.slots(2, [128, 2, 512], BF16, "ys")
        Vm_sb = self.slots(2, [128, 4, 128], BF16, "Vm")
        for vmt in Vm_sb:
            self.memset("pool", vmt, 0.0)
            for h in range(4):
                col = 64 if h % 2 == 0 else 0
                self.memset("dve", vmt[:, h, col:col + 1], 1.0)
        vet = self.slots(2, [128, 1], F32, "vet")
        rstdt = self.slots(2, [128, 1], F32, "rstdt")

        tpool = [0, 1]
        fpool = [2, 3, 4, 5]
        kpool = [6, 7]
        tcount = 0
        for gi, (t0, nt, s) in enumerate(self.groups(nlim=_DBG_NG)):
            ntok = nt * 128
            c0 = t0 * 128
            hT = hTs[gi % 2]
            rp = ropes[gi % 2]
            self.dma("sp", rp[:, :, 0:ntok], self.rope.v(lambda p: p[:, :, c0:c0 + ntok].rearrange("a p t -> p a t")),
                     "a_rope%d" % (gi % 2))
            for jl in range(nt):
                tg = t0 + jl
                xt = xts[tcount % 2]
                self.dma("sp", xt, xsrc[tg * 128:(tg + 1) * 128, :], "a_x%d" % (tcount % 2))
                self.ln_mod_T(xt, hT[:, :, jl * 128:(jl + 1) * 128], s, 1, 0, lnws[tcount % 2],
                              xns[tcount % 2], tmps[tcount % 2], tpool)
                tcount += 1

            def proj(coff, M):
                ps = self.bank(self.nb(fpool), [M, ntok])
                for k in range(8):
                    self.mm(ps, winx[:, k, coff:coff + M], hT[:, k, 0:ntok], k == 0, k == 7)
                return ps

            cos_s, sin_s, cos_m, sin_m = (rp[:, i, 0:ntok] for i in range(4))
            self.T.mute = not _dbgp("za")
            za = za_sb[gi % 2]
            for c in range(2):
                ps = proj(C_POOL + c * 128, 128)
                self.act(za[:, c, 0:ntok], ps, AF.Copy)
            zc0 = 8 + c0 if s == 0 else 4120
            self.dma("pool", self.zaT[:, :, zc0:zc0 + ntok], za[:, :, 0:ntok], "a_sza%d" % (gi % 2))
            self.T.mute = not _dbgp("q")
            qs = qs_sb[gi % 2]
            for hp in range(2):
                pm = proj(C_SQ + hp * 128, 128)
                t1 = t1s[hp]
                self.tt("dve", t1[:, 0:ntok], pm, cos_s, ALU.mult)
                pw = proj(C_SQS + hp * 128, 128)
                t2 = t2s[hp]
                self.tt("dve", t2[:, 0:ntok], pw, sin_s, ALU.mult)
                self.tt("dve", qs[:, hp, 0:ntok], t1[:, 0:ntok], t2[:, 0:ntok], ALU.add)
            self.dma("pool", self.qsT[:, :, c0:c0 + ntok], qs[:, :, 0:ntok], "a_sqs%d" % (gi % 2))
            self.T.mute = not _dbgp("k")
            ks = ks_sb[gi % 2]
            for gk in range(2):
                pm = proj(C_SK + gk * 128, 128)
                t1 = t1s[gk]
                self.tt("dve", t1[:, 0:ntok], pm, cos_s, ALU.mult)
                pw = proj(C_SKS + gk * 128, 128)
                t2 = t2s[gk]
                self.tt("dve", t2[:, 0:ntok], pw, sin_s, ALU.mult)
                self.tt("dve", ks[:, gk, 0:ntok], t1[:, 0:ntok], t2[:, 0:ntok], ALU.add)
            kslot = (1 + t0) * 128 if s == 0 else 34 * 128
            self.dma("pool", self.ksT[:, :, kslot:kslot + ntok], ks[:, :, 0:ntok], "a_sks%d" % (gi % 2))
            self.T.mute = not _dbgp("mq")
            for i in range(2):
                pm = proj(C_CQ + i * 128, 128)
                self.act(sq[:, i, 0:ntok], pm, AF.Square)
                self.ts("dve", cqg[:, i, 0:ntok], pm, cv[:, 2 + i:3 + i], ALU.mult)
            ps = self.bank(self.nb(fpool), [128, ntok])
            for i in range(2):
                self.mm(ps, self.onesq, sq[:, i, 0:ntok], i == 0, i == 1)
            self.ts("dve", veq[:, 0:ntok], ps, EPS, ALU.add)
            self.act(veq[:, 0:ntok], veq[:, 0:ntok], AF.Ln)
            self.act(rstdq[:, 0:ntok], veq[:, 0:ntok], AF.Exp, scale=-0.5)
            self.tt("dve", rcq[0:96, 0:ntok], rstdq[0:96, 0:ntok], cos_m[0:96], ALU.mult)
            self.tt("dve", rsq[0:96, 0:ntok], rstdq[0:96, 0:ntok], sin_m[0:96], ALU.mult)
            Qm = Qm_sb[gi % 2]
            for h in range(4):
                pq = self.bank(self.nb(fpool), [96, ntok])
                for kc in range(2):
                    self.mm(pq, wuq[:, kc, h * 96:(h + 1) * 96], cqg[:, kc, 0:ntok], kc == 0, kc == 1)
                t1 = t1s[h % 2]
                self.tt("dve", t1[0:96, 0:ntok], pq, rcq[0:96, 0:ntok], ALU.mult)
                pw = self.bank(self.nb(fpool), [96, ntok])
                for kc in range(2):
                    self.mm(pw, wuq[:, kc, 384 + h * 96:384 + (h + 1) * 96], cqg[:, kc, 0:ntok], kc == 0, kc == 1)
                t2 = t2s[h % 2]
                self.tt("dve", t2[0:96, 0:ntok], pw, rsq[0:96, 0:ntok], ALU.mult)
                self.tt("dve", Qm[:, h, 0:ntok], t1[0:96, 0:ntok], t2[0:96, 0:ntok], ALU.add)
            self.dma("pool", self.QmT[:, :, c0:c0 + ntok], Qm[:, :, 0:ntok], "a_sQm%d" % (gi % 2))
            self.T.mute = not _dbgp("u")
            u = uT[gi % 2]
            for i in range(2):
                pm = proj(C_U + i * 128, 128)
                self.act(u[:, i, 0:ntok], pm, AF.Gelu)
            self.T.mute = not _dbgp("kv")
            pm = proj(C_CKV, 128)
            self.act(sqk[:, 0:ntok], pm, AF.Square)
            self.ts("dve", ckvg[:, 0:ntok], pm, cv[:, 4:5], ALU.mult)
            ps = self.bank(self.nb(fpool), [128, ntok])
            self.mm(ps, self.oneskv, sqk[:, 0:ntok], True, True)
            self.ts("dve", vek[:, 0:ntok], ps, EPS, ALU.add)
            self.act(vek[:, 0:ntok], vek[:, 0:ntok], AF.Ln)
            self.act(rstdk[:, 0:ntok], vek[:, 0:ntok], AF.Exp, scale=-0.5)
            Km = Km_sb[gi % 2]
            for h in range(4):
                pk = self.bank(self.nb(fpool), [64, ntok])
                self.mm(pk, wukv[:, h * 64:(h + 1) * 64], ckvg[:, 0:ntok], True, True)
                self.tt("dve", Km[0:64, h, 0:ntok], pk, rstdk[0:64, 0:ntok], ALU.mult)
            pm = proj(C_KR, 96)
            t1 = t1s[0]
            self.tt("dve", t1[64:96, 0:ntok], pm[64:96], cos_m[64:96], ALU.mult)
            pw = proj(C_KRS, 96)
            t2 = t2s[0]
            self.tt("dve", t2[64:96, 0:ntok], pw[64:96], sin_m[64:96], ALU.mult)
            self.tt("dve", Km[64:96, :, 0:ntok],
                    t1.v(lambda p: p[64:96, 0:ntok].unsqueeze(1).broadcast_to([32, 4, ntok])),
                    t2.v(lambda p: p[64:96, 0:ntok].unsqueeze(1).broadcast_to([32, 4, ntok])), ALU.add)
            if s == 0:
                pj, c2 = t0 // 8, (t0 % 8) // 4
                self.dma("pool", self.KmT[pj].v(lambda p: p[:, c2 * 2048:(c2 + 1) * 2048].rearrange(
                    "d (h t) -> d h t", h=4)), Km[:, :, 0:ntok], "a_sKm%d" % (gi % 2))
            else:
                self.dma("pool", self.KmTc, Km[:, :, 0:ntok], "a_sKm%d" % (gi % 2))
            self.T.mute = not _dbgp("tok")
            ys = ys_sb[gi % 2]
            for jl in range(nt):
                tg = t0 + jl
                js = slice(jl * 128, (jl + 1) * 128)
                self.T.mute = not (_dbgp("tok") and _dbgp("tv"))
                pv = self.bank(self.nb(kpool), [128, 128])
                for k in range(8):
                    self.mm(pv, hT[:, k, js], winx[:, k, C_TV:C_TV + 128], k == 0, k == 7)
                vsb = vs_sb[jl % 2]
                self.act(vsb, pv, AF.Copy)
                vrow = (1 + tg) * 128 if s == 0 else (34 + jl) * 128
                self.dma("pool", self.vs[vrow:vrow + 128, :], vsb, "a_svs%d" % (jl % 2))
                self.T.mute = not (_dbgp("tok") and _dbgp("tg"))
                pg = self.bank(self.nb(kpool), [128, 256])
                for k in range(8):
                    self.mm(pg, hT[:, k, js], winx[:, k, C_TS:C_TS + 256], k == 0, k == 7)
                vgt = vg[jl % 2]
                self.act(vgt, pg, AF.Gelu)
                rstd, nb = self.ln_stats(vgt, 256, lnw2[jl % 2])
                vnt = vn[jl % 2]
                self.act(vnt, vgt, AF.Identity, bias=nb, scale=rstd)
                self.tt("dve", vnt, vnt, sgg, ALU.mult)
                vp = vpad[jl % 2]
                for hh in range(4):
                    co = (hh % 2) * 64
                    self.tt("dve", vp[:, hh, co:co + 64], vnt[:, hh * 64:(hh + 1) * 64], sgb[:, hh * 64:(hh + 1) * 64],
                            ALU.add)
                for pr in range(2):
                    pmx = self.bank(self.nb(kpool), [128, 128])
                    self.mm(pmx, vp[:, 2 * pr, :], sguw[:, 2 * pr, :], True, False)
                    self.mm(pmx, vp[:, 2 * pr + 1, :], sguw[:, 2 * pr + 1, :], False, True)
                    tsgt = tsg[pr]
                    self.tt("dve", tsgt, pmx, bsT[:, pr, :], ALU.add)
                    self.tt("dve", ys[:, pr, js], tsgt, u[:, pr, js], ALU.mult)
                self.T.mute = not (_dbgp("tok") and _dbgp("tm"))
                pvm = self.bank(self.nb(kpool), [128, 257])
                self.mm(pvm[:, 0:256], ckvg[:, js], wukv[:, 256:512], True, True)
                self.mm(pvm[:, 256:257], sqk[:, js], self.oneskv[:, 0:1], False, True, sg=True)
                self.ts("dve", vet[jl % 2], pvm[:, 256:257], EPS, ALU.add)
                self.act(vet[jl % 2], vet[jl % 2], AF.Ln)
                self.act(rstdt[jl % 2], vet[jl % 2], AF.Exp, scale=-0.5)
                vmt = Vm_sb[jl % 2]
                for h in range(4):
                    co = 0 if h % 2 == 0 else 64
                    self.act(vmt[:, h, co:co + 64], pvm[:, h * 64:(h + 1) * 64], AF.Identity, scale=rstdt[jl % 2])
                vm_dst = (self.Vm[tg // 8][:, (tg % 8) * 512:(tg % 8 + 1) * 512] if s == 0
                          else self.Vmc[jl * 128:(jl + 1) * 128, :])
                self.dma("pool", vm_dst, vmt.v(lambda p: p.rearrange("p h c -> p (h c)")), "a_sVm%d" % (jl % 2))
            self.T.mute = not _dbgp("tok")
            self.dma("pool", self.ysT[:, :, c0:c0 + ntok], ys[:, :, 0:ntok], "a_sys%d" % (gi % 2))
            self.T.mute = False
            if s == 0 and t0 % 8 == 4:
                pj = t0 // 8
                self.coll(self.KmT_all[pj], self.KmT[pj], "x_ck%d" % pj)
                self.coll(self.Vm_all[pj], self.Vm[pj], "x_cv%d" % pj)

    def coll(self, out, in_, key):
        rg = [[0, 1, 2, 3], [4, 5, 6, 7]]
        if _DBG_MOCKX:
            rows = in_.ap.shape[0]
            for rr in range(4):
                self.dma("pool", out[rr * rows:(rr + 1) * rows], in_, "mock%d" % rr)
            return
        self.T.op("pool", lambda e, o=out.ap, i=in_.ap: e.collective_compute(
            "AllGather", ALU.bypass, replica_groups=rg, ins=[i.opt()], outs=[o.opt()]),
            reads=[in_.buf], writes=[out.buf], dma_key="CC_" + key, inc=1)

    def phase_X(self, li):
        self.phase()
        a = self.alloc
        hk = self.hk_in.v(lambda p: p.rearrange("p (f g t) -> p f g t", f=2, g=2))
        self.dma("pool", hk[:, 0], self.ksT[:, :, 128:256], "x_h0")
        self.dma("pool", hk[:, 1], self.ksT[:, :, 32 * 128:33 * 128], "x_h1")
        self.dma("pool", self.hv_in[0:128, :], self.vs[128:256, :], "x_h2")
        self.dma("pool", self.hv_in[128:256, :], self.vs[32 * 128:33 * 128, :], "x_h3")
        hz = self.hz_in.v(lambda p: p.rearrange("p (c f t) -> p c f t", c=2, f=2))
        self.dma_nc("pool", hz[:, :, 0, :], self.zaT[:, :, 8:16], "x_h4")
        self.dma_nc("pool", hz[:, :, 1, :], self.zaT[:, :, 4096:4104], "x_h5")
        self.coll(self.hk_all, self.hk_in, "x_c2")
        self.coll(self.hv_all, self.hv_in, "x_c3")
        self.coll(self.hz_all, self.hz_in, "x_c4")
        hka = a([128, 4, 512], BF16, "hka")
        self.dma("sp", hka, self.hk_all.v(lambda p: p.rearrange("(r p) c -> p r c", p=128)), "x_l0")
        hva = a([128, 4, 2, 128], BF16, "hva")
        self.dma("sp", hva, self.hv_all.v(lambda p: p.rearrange("(r f p) c -> p r f c", r=4, f=2)), "x_l1")
        hza = a([128, 4, 32], F32, "hza")
        self.dma("sp", hza, self.hz_all.v(lambda p: p.rearrange("(r p) c -> p r c", p=128)), "x_l2")
        sel = self.sel_sb

        def pick(dst, src_fn, soff):
            for rr in range(4):
                sc = sel[:, soff + rr:soff + rr + 1]
                if rr == 0:
                    self.ts("dve", dst, src_fn(rr), sc, ALU.mult)
                else:
                    self.stt(dst, src_fn(rr), sc, dst, ALU.mult, ALU.add)
        kp = a([128, 256], BF16, "kp")
        kn = a([128, 256], BF16, "kn")
        pick(kp, lambda rr: hka[:, rr, 256:512], 0)
        pick(kn, lambda rr: hka[:, rr, 0:256], 4)
        self.dma("pool", self.ksT[:, :, 0:128], kp.v(lambda p: p.rearrange("p (g t) -> p g t", g=2)), "x_s0")
        self.dma("pool", self.ksT[:, :, 33 * 128:34 * 128], kn.v(lambda p: p.rearrange("p (g t) -> p g t", g=2)), "x_s1")
        vp = a([128, 128], BF16, "vp")
        vn = a([128, 128], BF16, "vn")
        pick(vp, lambda rr: hva[:, rr, 1, :], 0)
        pick(vn, lambda rr: hva[:, rr, 0, :], 4)
        self.dma("pool", self.vs[0:128, :], vp, "x_s2")
        self.dma("pool", self.vs[33 * 128:34 * 128, :], vn, "x_s3")
        zp = a([128, 2, 8], F32, "zp")
        zn = a([128, 2, 8], F32, "zn")
        hz4 = hza.v(lambda p: p.rearrange("p r (c f t) -> p r c f t", c=2, f=2))
        pick(zp, lambda rr: hz4[:, rr, :, 1, :], 0)
        pick(zn, lambda rr: hz4[:, rr, :, 0, :], 4)
        self.dma_nc("pool", self.zaT[:, :, 0:8], zp, "x_s4")
        self.dma_nc("pool", self.zaT[:, :, 4104:4112], zn, "x_s5")

    def phase_B(self, li, xsrc, do_ctx):
        self.phase()
        a = self.alloc
        wout = a([128, 8, D], BF16, "wout")
        self.dma("sp", wout, self.wb_out.v(lambda p: p[li].rearrange("(k p) n -> p k n", p=128)), "b_wout")
        pbd = a([128, 2, 128], BF16, "pbd")
        self.dma("sp", pbd, self.wb_pool.v(lambda p: p[li].rearrange("(c p) n -> p c n", p=128)), "b_pbd")
        cv = a([128, 8], F32, "cv")
        self.dma("sp", cv, self.colvec.v(lambda p: p[li]), "b_cv")
        esink = a([128, 2], F32, "esink")
        self.act(esink, cv[:, 5:7], AF.Exp)
        m2bc = [a([128, D], F32, "m2bc") for _ in range(2)]
        self.dma("sp", m2bc[0], self.mrow_bc(li, 0, 2), "b_m2a")
        if do_ctx:
            self.dma("sp", m2bc[1], self.mrow_bc(li, 1, 2), "b_m2b")
        g1 = a([128, D], F32, "g1")
        b1 = a([128, D], F32, "b1")
        self.dma("sp", g1, self.rowvec.v(lambda p: p[li, 2:3, :].partition_broadcast(128)), "b_g1")
        self.dma("sp", b1, self.rowvec.v(lambda p: p[li, 3:4, :].partition_broadcast(128)), "b_b1")
        ksc = a([128, 2, 256], BF16, "ksc")
        self.dma("sp", ksc, self.ksT[:, :, 34 * 128:36 * 128], "b_ksc")
        vsc = a([128, 2, 2, 2, 128], BF16, "vsc") if False else None
        vscA = a([128, 2, 2, 128], BF16, "vscA")
        vscB = a([128, 2, 2, 128], BF16, "vscB")
        self.memset("pool", vscA, 0.0)
        self.memset("pool", vscB, 0.0)
        vsrc = self.vs.v(lambda p: p[34 * 128:36 * 128, :].rearrange("(t p) (g c) -> p t g c", p=128, g=2))
        for g_ in range(2):
            self.dma_nc("sp", vscA[:, :, g_, 0:64], vsrc[:, :, g_, :], "b_vscA%d" % g_)
            self.dma_nc("sp", vscB[:, :, g_, 64:128], vsrc[:, :, g_, :], "b_vscB%d" % g_)
        Kmc = a([96, 4, 256], BF16, "Kmc")
        self.dma("sp", Kmc, self.KmTc, "b_Kmc")
        Vmc = a([128, 2, 512], BF16, "Vmc")
        self.dma("sp", Vmc, self.Vmc.v(lambda p: p.rearrange("(t p) c -> p t c", p=128)), "b_Vmc")
        zw = self.slots(2, [128, 2, 528], F32, "zw")
        rc = self.slots(2, [128, 2, 512], F32, "rc")
        S2 = a([128, 2, 528], F32, "S2")
        S4 = a([128, 2, 528], F32, "S4")
        S8 = a([128, 528], F32, "S8")
        S16 = a([128, 528], F32, "S16")
        tq = a([128, 2, 512], F32, "tq")
        dT = a([128, 2, 512], BF16, "dT")
        yT = self.slots(2, [128, 8, 512], BF16, "yT")
        qzA = self.slots(2, [128, 2, 512], BF16, "qzA")
        qzB = self.slots(2, [128, 2, 512], BF16, "qzB")
        for q_ in qzA + qzB:
            self.memset("pool", q_, 0.0)
        ksw = self.slots(2, [128, 2, 768], BF16, "ksw")
        vswA = self.slots(2, [128, 6, 2, 128], BF16, "vswA")
        vswB = self.slots(2, [128, 6, 2, 128], BF16, "vswB")
        for v_ in vswA + vswB:
            self.memset("pool", v_, 0.0)
        PT = self.slots(4, [128, 512], BF16, "PT")
        PT2 = self.slots(3, [128, 2, 512], BF16, "PT2")
        dsum = self.slots(2, [128, 256], F32, "dsum")
        QT = self.slots(2, [96, 4, 512], BF16, "QT")
        KT = self.slots(3, [96, 4, 512], BF16, "KT")
        VT = self.slots(3, [128, 4, 512], BF16, "VT")
        Osb = self.slots(2, [128, 512], F32, "Osb")
        rrow = self.slots(2, [128, 512], F32, "rrow")
        xts = self.slots(2, [128, D], F32, "xt")
        tts = self.slots(2, [128, D], F32, "tt")
        xns = self.slots(2, [128, D], F32, "xn")
        lnws = [self.ln_work() for _ in range(2)]

        opool = [0, 1, 2, 3]
        spool = [4, 5, 6, 7]
        tcount = 0
        for gi, (t0, nt, s) in enumerate(self.groups(with_ctx=do_ctx, nlim=_DBG_NGB)):
            ntok = nt * 128
            c0 = t0 * 128
            y = yT[gi % 2]
            W = ntok + 16
            zc0 = c0 if s == 0 else 4112
            z = zw[gi % 2]
            self.dma("sp", z[:, :, 0:W], self.zaT[:, :, zc0:zc0 + W], "b_zw%d" % (gi % 2))
            r = rc[gi % 2]
            self.dma("sp", r[:, :, 0:ntok], self.rcnt[:, :, c0:c0 + ntok], "b_rc%d" % (gi % 2))
            self.tt("dve", S2[:, :, 1:W], z[:, :, 0:W - 1], z[:, :, 1:W], ALU.add)
            self.tt("dve", S4[:, :, 2:W - 1], S2[:, :, 1:W - 2], S2[:, :, 3:W], ALU.add)
            self.tt("dve", S8[:, 4:W - 3], S4[:, 1, 2:W - 5], S4[:, 1, 6:W - 1], ALU.add)
            self.tt("dve", S16[64:128, 8:W - 7], S8[64:128, 4:W - 11], S8[64:128, 12:W - 3], ALU.add)
            quads = [(0, 0, S2[0:64, 0, 8:8 + ntok]), (0, 64, S4[64:128, 0, 8:8 + ntok]),
                     (1, 0, S8[0:64, 8:8 + ntok]), (1, 64, S16[64:128, 8:8 + ntok])]
            for (c, p0, Sv) in quads:
                self.tt("dve", tq[p0:p0 + 64, c, 0:ntok], Sv, r[p0:p0 + 64, c, 0:ntok], ALU.mult)
                self.tt("dve", dT[p0:p0 + 64, c, 0:ntok], tq[p0:p0 + 64, c, 0:ntok], z[p0:p0 + 64, c, 8:8 + ntok],
                        ALU.subtract)
            for c in range(2):
                ps = self.bank(self.nb(spool), [128, ntok])
                self.mm(ps, pbd[:, c, :], dT[:, c, 0:ntok], True, True)
                self.act(y[:, c, 0:ntok], ps, AF.Identity, scale=cv[:, c:c + 1])
            self.dma("sp", y[:, 6:8, 0:ntok], self.ysT[:, :, c0:c0 + ntok], "b_ys%d" % (gi % 2))
            qA = qzA[gi % 2]
            qB = qzB[gi % 2]
            self.dma("sp", qA[0:64, :, 0:ntok], self.qsT[0:64, :, c0:c0 + ntok], "b_qsA%d" % (gi % 2))
            self.dma("sp", qB[64:128, :, 0:ntok], self.qsT[64:128, :, c0:c0 + ntok], "b_qsB%d" % (gi % 2))
            if s == 0:
                kw = ksw[gi % 2]
                self.dma("sp", kw, self.ksT[:, :, c0:c0 + 768], "b_kw%d" % (gi % 2))
                vA = vswA[gi % 2]
                vB = vswB[gi % 2]
                vsrc = self.vs.v(lambda p: p[c0:c0 + 768, :].rearrange("(t p) (g c) -> p t g c", p=128, g=2))
                for g_ in range(2):
                    self.dma_nc("sp", vA[:, :, g_, 0:64], vsrc[:, :, g_, :], "b_vA%d_%d" % (gi % 2, g_))
                    self.dma_nc("sp", vB[:, :, g_, 64:128], vsrc[:, :, g_, :], "b_vB%d_%d" % (gi % 2, g_))
            for jl in range(nt):
                tg = t0 + jl
                js = slice(jl * 128, (jl + 1) * 128)
                if s == 0:
                    keys = [(kw[:, :, jl * 128:(jl + 1) * 128], vA[:, jl], vB[:, jl], 0 if tg == 0 else 1),
                            (kw[:, :, (jl + 1) * 128:(jl + 2) * 128], vA[:, jl + 1], vB[:, jl + 1], None),
                            (kw[:, :, (jl + 2) * 128:(jl + 3) * 128], vA[:, jl + 2], vB[:, jl + 2],
                             3 if tg == NTL - 1 else 2)]
                else:
                    keys = []
                keys += [(ksc[:, :, 0:128], vscA[:, 0], vscB[:, 0], None),
                         (ksc[:, :, 128:256], vscA[:, 1], vscB[:, 1], None)]
                ob = self.bank(self.nb(opool), [128, 4, 128])
                nk = len(keys)
                for ki, (kt, va, vb, mk) in enumerate(keys):
                    sb_ = self.bank(self.nb(spool), [128, 4, 128])
                    for hd in range(4):
                        gk, e = hd // 2, hd % 2
                        self.mm(sb_[:, hd, :], kt[:, gk, :], (qA if e == 0 else qB)[:, gk, js], hd == 0, hd == 3, sg=True)
                    pt = PT[(tcount * 8 + ki) % 4]
                    ptv = pt.v(lambda p: p.rearrange("p (h t) -> p h t", h=4))
                    self.act(ptv, sb_, AF.Exp, scale=SWA_SCALE)
                    if mk is not None:
                        self.tt("dve", ptv, ptv, self.masks_sb.v(
                            lambda p: p[:, mk, :].unsqueeze(1).broadcast_to([128, 4, 128])), ALU.mult)
                    for gk in range(2):
                        first = (ki == 0 and gk == 0)
                        lastk = (ki == nk - 1)
                        self.mm(ob[:, gk, :], va[:, gk, :], ptv[:, 2 * gk, :], first, False, sg=True)
                        self.mm(ob[:, gk, :], vb[:, gk, :], ptv[:, 2 * gk + 1, :], False, lastk, sg=True)
                        self.mm(ob[:, 2 + gk, :], self.onesA, ptv[:, 2 * gk, :], False, False, sg=True)
                        self.mm(ob[:, 2 + gk, :], self.onesB, ptv[:, 2 * gk + 1, :], False, lastk, sg=True)
                ds_ = dsum[tcount % 2]
                for gk in range(2):
                    self.ts("dve", ds_[:, gk * 128:(gk + 1) * 128], ob[:, 2 + gk, :], esink[:, gk:gk + 1], ALU.add)
                self.recip(ds_, ds_)
                self.tt("dve", y[:, 2:4, js], ob[:, 0:2, :], ds_.v(lambda p: p.rearrange("p (g t) -> p g t", g=2)),
                        ALU.mult)
                tcount += 1
            Q = QT[gi % 2]
            self.dma("sp", Q[:, :, 0:ntok], self.QmT[:, :, c0:c0 + ntok], "b_Q%d" % (gi % 2))
            chunks = []
            if s == 0:
                for rr in range(4):
                    for ch in range(_DBG_NCH):
                        chunks.append((rr, ch))
            chunks.append(None)
            items = []
            nload = [0]

            def load_chunk(ci):
                ck = chunks[ci]
                if ck is None:
                    return Kmc, Vmc, 2
                rr, ch = ck
                sl = nload[0] % 3
                nload[0] += 1
                K_ = KT[sl]
                V_ = VT[sl]
                pj, c2 = ch // 2, ch % 2
                self.dma("sp", K_, self.KmT_all[pj].v(lambda p: p[rr * 96:(rr + 1) * 96, c2 * 2048:(c2 + 1) * 2048].rearrange(
                    "d (h t) -> d h t", h=4)), "b_K%d" % sl)
                self.dma("sp", V_, self.Vm_all[pj].v(lambda p: p[rr * 128:(rr + 1) * 128, c2 * 2048:(c2 + 1) * 2048].rearrange(
                    "p (t c) -> p t c", t=4)), "b_V%d" % sl)
                return K_, V_, 4
            loaded = {}
            obanks = [self.bank(i, [128, ntok]) for i in range(4)]
            sp2 = [4, 5, 6, 7]
            seqp = []
            for ci in range(len(chunks)):
                nkt = 2 if chunks[ci] is None else 4
                for kt in range(nkt):
                    for hp in range(2):
                        seqp.append((ci, kt, hp))
            LAp = 1
            sb_of = {}
            pairpool = [4, 6]

            def issue_Sp(j):
                ci, kt, hp = seqp[j]
                K_, V_, _ = loaded[ci]
                b0 = pairpool.pop(0)
                pairpool.append(b0)
                for e in range(2):
                    h = 2 * hp + e
                    self.mm(self.bank(b0 + e, [128, ntok]), K_[:, h, kt * 128:(kt + 1) * 128], Q[:, h, 0:ntok], True, True)
                sb_of[j] = b0
            for cj in range(min(3, len(chunks))):
                loaded[cj] = load_chunk(cj)
            for j in range(min(LAp, len(seqp))):
                issue_Sp(j)
            for j in range(len(seqp)):
                if j + LAp < len(seqp):
                    issue_Sp(j + LAp)
                ci, kt, hp = seqp[j]
                K_, V_, _ = loaded[ci]
                b0 = sb_of.pop(j)
                pt = PT2[j % 3]
                src = self.psum_all[:, b0 * 2048:(b0 + 2) * 2048].bitcast(F32).rearrange("p (a b) -> p a b", a=2)[:, :, 0:ntok]
                self.T.op("act", lambda e_, o=pt.ap[:, :, 0:ntok], i_=src: e_.activation(
                    out=o, in_=i_, func=AF.Exp, scale=MLA_SCALE),
                    reads=[self.bbuf[b0], self.bbuf[b0 + 1]], writes=[pt.buf])
                for e in range(2):
                    h = 2 * hp + e
                    first = (j < 2)
                    lastk = (j >= len(seqp) - 2)
                    self.mm(obanks[h], V_[:, kt, h * 128:(h + 1) * 128], pt[:, e, 0:ntok], first, lastk)
                if (j + 1 == len(seqp) or seqp[j + 1][0] != ci) and ci + 3 < len(chunks):
                    loaded[ci + 3] = load_chunk(ci + 3)
            for h in range(4):
                e = h % 2
                dp = 64 if e == 0 else 0
                rows = slice(0, 64) if e == 0 else slice(64, 128)
                osb = Osb[h % 2]
                self.act(osb[:, 0:ntok], obanks[h], AF.Copy)
                rr_ = rrow[h % 2]
                self.recip(rr_[dp:dp + 1, 0:ntok], osb[dp:dp + 1, 0:ntok])
                pb = self.bank(self.nb(sp2), [128, ntok])
                self.mm(pb, self.ones_f[dp:dp + 1, :], rr_[dp:dp + 1, 0:ntok], True, True)
                self.tt("dve", y[rows, 4 + h // 2, 0:ntok], osb[rows, 0:ntok], pb[rows], ALU.mult)
            for jl in range(nt):
                tg = t0 + jl
                js = slice(jl * 128, (jl + 1) * 128)
                xt = xts[tcount % 2]
                self.dma("sp", xt, xsrc[tg * 128:(tg + 1) * 128, :], "b_x%d" % (tcount % 2))
                t_ = tts[tcount % 2]
                for hf in range(2):
                    ps = self.bank(self.nb(sp2), [128, 512])
                    for k in range(8):
                        self.mm(ps, y[:, k, js], wout[:, k, hf * 512:(hf + 1) * 512], k == 0, k == 7)
                    self.tt("dve", t_[:, hf * 512:(hf + 1) * 512], ps, m2bc[s][:, hf * 512:(hf + 1) * 512], ALU.mult)
                self.stt(t_, xt, ALPHA, t_, ALU.mult, ALU.add)
                rstd, nb = self.ln_stats(t_, D, lnws[tcount % 2])
                xn = xns[tcount % 2]
                self.act(xn, t_, AF.Identity, bias=nb, scale=rstd)
                self.tt("dve", xn, xn, g1, ALU.mult)
                self.tt("dve", xn, xn, b1, ALU.add)
                self.dma("pool", self.x1s[tg * 128:(tg + 1) * 128, :], xn, "b_sx%d" % (tcount % 2))
                tcount += 1

    def phase_C(self, li, do_ctx, final):
        self.phase()
        a = self.alloc
        wf1 = a([128, 8, 4 * D], BF16, "wf1")
        for c in range(4):
            self.dma("sp", wf1[:, 2 * c:2 * c + 2, :], self.wb_ff1.v(
                lambda p: p[li, c * 256:(c + 1) * 256, :].rearrange("(k p) n -> p k n", p=128)), "c_w1%d" % c)
        wf2 = a([128, 32, D], BF16, "wf2")
        for c in range(4):
            self.dma("sp", wf2[:, 8 * c:8 * c + 8, :], self.wb_ff2.v(
                lambda p: p[li, c * 1024:(c + 1) * 1024, :].rearrange("(k p) n -> p k n", p=128)), "c_w2%d" % c)
        m5bc = [a([128, D], F32, "m5bc") for _ in range(2)]
        self.dma("sp", m5bc[0], self.mrow_bc(li, 0, 5), "c_m5a")
        if do_ctx:
            self.dma("sp", m5bc[1], self.mrow_bc(li, 1, 5), "c_m5b")
        g2 = a([128, D], F32, "g2")
        b2 = a([128, D], F32, "b2")
        self.dma("sp", g2, self.rowvec.v(lambda p: p[li, 4:5, :].partition_broadcast(128)), "c_g2")
        self.dma("sp", b2, self.rowvec.v(lambda p: p[li, 5:6, :].partition_broadcast(128)), "c_b2")
        x1t = self.slots(3, [128, D], F32, "x1t")
        xnb = self.slots(2, [128, D], BF16, "xnb")
        tmps = self.slots(1, [128, 8, 128], F32, "tmp")
        h2T = self.slots(1, [128, 8, 256], BF16, "h2T")[0]
        gT = self.slots(1, [128, 32, 256], BF16, "gT")[0]
        rl = self.slots(2, [128, 256], F32, "rl")
        tts = self.slots(2, [128, D], F32, "tt")
        lnws = [self.ln_work() for _ in range(2)]
        lnw3 = [self.ln_work() for _ in range(2)]
        tpool = [0, 1]
        fpool = [2, 3]
        opool = [4, 5, 6, 7]
        tcount = 0
        for gi, (t0, nt, s) in enumerate(self.groups(with_ctx=do_ctx, gsz=2, nlim=(None if _DBG_NGB is None else 2 * _DBG_NGB))):
            ntok = nt * 128
            xl = []
            for jl in range(nt):
                tg = t0 + jl
                xt = x1t[tcount % 3]
                xl.append(xt)
                self.dma("sp", xt, self.x1s[tg * 128:(tg + 1) * 128, :], "c_x%d" % (tcount % 3))
                self.ln_mod_T(xt, h2T[:, :, jl * 128:(jl + 1) * 128], s, 4, 3, lnws[tcount % 2],
                              xnb[tcount % 2], tmps[0], tpool)
                tcount += 1
            for n in range(32):
                ps = self.bank(self.nb(fpool), [128, ntok])
                for k in range(8):
                    self.mm(ps, wf1[:, k, n * 128:(n + 1) * 128], h2T[:, k, 0:ntok], k == 0, k == 7)
                r_ = rl[n % 2]
                self.act(r_[:, 0:ntok], ps, AF.Relu)
                self.tt("dve", gT[:, n, 0:ntok], r_[:, 0:ntok], ps, ALU.mult)
            for jl in range(nt):
                tg = t0 + jl
                js = slice(jl * 128, (jl + 1) * 128)
                t_ = tts[jl % 2]
                xt = xl[jl]
                for hf in range(2):
                    ps = self.bank(self.nb(opool), [128, 512])
                    for n in range(32):
                        self.mm(ps, gT[:, n, js], wf2[:, n, hf * 512:(hf + 1) * 512], n == 0, n == 31)
                    self.tt("dve", t_[:, hf * 512:(hf + 1) * 512], ps, m5bc[s][:, hf * 512:(hf + 1) * 512], ALU.mult)
                self.stt(t_, xt, ALPHA, t_, ALU.mult, ALU.add)
                rstd, nb = self.ln_stats(t_, D, lnw3[jl % 2])
                self.act(xt, t_, AF.Identity, bias=nb, scale=rstd)
                self.tt("dve", xt, xt, g2, ALU.mult)
                self.tt("dve", xt, xt, b2, ALU.add)
                if final:
                    if s == 0:
                        dst = self.y[tg * 128:(tg + 1) * 128, :]
                    else:
                        dst = self.yc[jl * 128:(jl + 1) * 128, :]
                else:
                    dst = self.xs[tg * 128:(tg + 1) * 128, :]
                self.dma("pool", dst, xt, "c_sx%d" % ((tcount - nt + jl) % 3))


def _rope_tables(r):
    t = np.arange(r * TOWN, (r + 1) * TOWN)
    row = (t // 64).astype(np.float32)
    col = (t % 64).astype(np.float32)

    def tab(d_rot, nrows, base):
        d_ax = d_rot // 2
        inv = (np.float32(10000.0) ** (-np.arange(0, d_ax, 2, dtype=np.float32) / np.float32(d_ax))).astype(np.float32)
        nf = d_ax // 2
        ar = (row[:, None] * inv[None, :]).astype(np.float32)
        ac = (col[:, None] * inv[None, :]).astype(np.float32)
        cos = np.ones((nrows, NTOK), np.float32)
        sin = np.zeros((nrows, NTOK), np.float32)
        blocks = [(ar, -1.0), (ar, 1.0), (ac, -1.0), (ac, 1.0)]
        for bi, (ang, sg) in enumerate(blocks):
            lo = base + bi * nf
            cos[lo:lo + nf, :TOWN] = np.cos(ang).T
            sin[lo:lo + nf, :TOWN] = sg * np.sin(ang).T
        return cos, sin
    cs, ss = tab(64, 128, 0)
    cs[64:128] = cs[0:64]
    ss[64:128] = ss[0:64]
    cm, sm = tab(32, 128, 64)
    return np.stack([cs, ss, cm, sm]).astype(np.float32)


def _rcnt_table(r):
    out = np.zeros((128, 2, NTOK), np.float32)
    wins = (2, 4, 8, 16)
    t = np.arange(r * TOWN, (r + 1) * TOWN)
    tc = np.arange(NCTX)
    for gi, w in enumerate(wins):
        c, p0 = gi // 2, (gi % 2) * 64
        cnt = np.clip(t + w // 2, 0, 4 * TOWN) - np.clip(t - w // 2, 0, 4 * TOWN)
        out[p0:p0 + 64, c, :TOWN] = (1.0 / cnt.astype(np.float32))[None, :]
        cntc = np.clip(tc + w // 2, 0, NCTX) - np.clip(tc - w // 2, 0, NCTX)
        out[p0:p0 + 64, c, TOWN:] = (1.0 / cntc.astype(np.float32))[None, :]
    return out


def _masks(r):
    k = np.arange(128)[:, None]
    q = np.arange(128)[None, :]
    prev = (k >= q).astype(np.float32)
    nxt = (k <= q).astype(np.float32)
    z = np.zeros_like(prev)
    m = np.stack([z if r == 0 else prev, prev, nxt, z if r == 3 else nxt], axis=1)
    return m.astype(ml_dtypes.bfloat16)


def _layer_inputs(p, ls):
    nl = len(ls)
    w_in = p["w_in"][ls]
    swq = np.array([h * 64 + j for h in range(4) for j in (list(range(16, 32)) + list(range(0, 16)) +
                                                           list(range(48, 64)) + list(range(32, 48)))])
    swk = np.array([g * 64 + j for g in range(2) for j in (list(range(16, 32)) + list(range(0, 16)) +
                                                           list(range(48, 64)) + list(range(32, 48)))])
    swr = np.array(list(range(8, 16)) + list(range(0, 8)) + list(range(24, 32)) + list(range(16, 24)))
    winx = np.zeros((nl, D, NIN), np.float32)
    winx[:, :, C_POOL:C_POOL + 256] = w_in[:, :, 0:256]
    winx[:, :, C_SQ:C_SQ + 256] = w_in[:, :, 256:512]
    winx[:, :, C_SQS:C_SQS + 256] = w_in[:, :, 256 + swq]
    winx[:, :, C_CQ:C_CQ + 256] = w_in[:, :, 512:768]
    winx[:, :, C_U:C_U + 256] = w_in[:, :, 768:1024]
    for g in range(2):
        kc = w_in[:, :, 1280 + g * 64:1280 + (g + 1) * 64]
        kcs = w_in[:, :, 1280 + swk[g * 64:(g + 1) * 64]]
        for e in range(2):
            winx[:, :, C_SK + g * 128 + e * 64:C_SK + g * 128 + (e + 1) * 64] = kc
            winx[:, :, C_SKS + g * 128 + e * 64:C_SKS + g * 128 + (e + 1) * 64] = kcs
    winx[:, :, C_CKV:C_CKV + 128] = w_in[:, :, 1536:1664]
    winx[:, :, C_KR + 64:C_KR + 96] = w_in[:, :, 1664:1696]
    winx[:, :, C_KRS + 64:C_KRS + 96] = w_in[:, :, 1664 + swr]
    winx[:, :, C_TV:C_TV + 128] = w_in[:, :, 1408:1536]
    winx[:, :, C_TS:C_TS + 256] = w_in[:, :, 1024:1280]
    wuq = p["mla_w_uq"][ls]
    wuqx = np.zeros((nl, 256, 768), np.float32)
    for h in range(4):
        wuqx[:, :, h * 96:(h + 1) * 96] = wuq[:, :, h * 96:(h + 1) * 96]
        wuqx[:, :, 384 + h * 96 + 64:384 + (h + 1) * 96] = wuq[:, :, h * 96 + 64 + swr]
    wukv = p["mla_w_ukv"][ls]
    wukvx = np.zeros((nl, 128, 512), np.float32)
    for h in range(4):
        wukvx[:, :, h * 64:(h + 1) * 64] = wukv[:, :, h * 128:h * 128 + 64]
        wukvx[:, :, 256 + h * 64:256 + (h + 1) * 64] = wukv[:, :, h * 128 + 64:(h + 1) * 128]
    pw = p["pool_w"][ls]
    pbd = np.zeros((nl, 2, 128, 128), np.float32)
    for c in range(2):
        pbd[:, c, 0:64, 0:64] = pw[:, 2 * c]
        pbd[:, c, 64:128, 64:128] = pw[:, 2 * c + 1]
    sguwT = np.ascontiguousarray(np.transpose(p["sgu_w"][ls], (0, 1, 3, 2))).reshape(nl, 512, 128)
    colvec = np.zeros((nl, 128, 8), np.float32)
    colvec[:, :, 0:2] = p["pool_scale"][ls].reshape(nl, 2, 128).transpose(0, 2, 1)
    colvec[:, :, 2:4] = p["mla_q_norm"][ls].reshape(nl, 2, 128).transpose(0, 2, 1)
    colvec[:, :, 4] = p["mla_kv_norm"][ls]
    sink = p["swa_sink"][ls]
    for gk in range(2):
        for e in range(2):
            colvec[:, e * 64:(e + 1) * 64, 5 + gk] = sink[:, 2 * gk + e][:, None]
    sb = p["sgu_b"][ls]
    bsT = np.zeros((nl, 128, 2, 128), np.float32)
    for pr in range(2):
        for e in range(2):
            bsT[:, e * 64:(e + 1) * 64, pr, :] = sb[:, 2 * pr + e][:, None, :]
    rowvec = np.zeros((nl, 6, D), np.float32)
    rowvec[:, 0, 0:256] = p["sgu_norm_g"][ls]
    rowvec[:, 0, 256:512] = p["sgu_norm_b"][ls]
    rowvec[:, 2] = p["ln1_g"][ls]
    rowvec[:, 3] = p["ln1_b"][ls]
    rowvec[:, 4] = p["ln2_g"][ls]
    rowvec[:, 5] = p["ln2_b"][ls]
    return dict(w_ada=np.ascontiguousarray(p["w_ada"][ls]), b_ada=np.ascontiguousarray(p["b_ada"][ls]),
                w_inx=winx, w_uqx=wuqx, w_ukvx=wukvx, pool_bd=pbd.reshape(nl, 256, 128), sgu_wT=sguwT,
                w_out=np.ascontiguousarray(p["w_out"][ls]), w_ff1=np.ascontiguousarray(p["w_ff1"][ls]),
                w_ff2=np.ascontiguousarray(p["w_ff2"][ls]), colvec=colvec, bsT=bsT.reshape(nl, 128, 256),
                rowvec=rowvec)


_CACHE = {}
import os as _os
_DBG_PARTS = _os.environ.get("DBG_A")
_DBG_NG = int(_os.environ["DBG_NG"]) if "DBG_NG" in _os.environ else None
_DBG_NGB = int(_os.environ["DBG_NGB"]) if "DBG_NGB" in _os.environ else None
_DBG_MOCKX = bool(int(_os.environ.get("DBG_MOCKX", "0")))
_DBG_NCH = int(_os.environ.get("DBG_NCH", "8"))
_DBG_SMALL = bool(int(_os.environ.get("DBG_SMALL", "0")))


def _dbgp(name):
    return _DBG_PARTS is None or name in _DBG_PARTS.split(",")

_STOP = None


def _program(layers, want_ctx_out):
    key = (tuple(layers), want_ctx_out)
    if key not in _CACHE:
        b = Builder(layers, want_ctx_out)
        if _STOP is not None:
            b.stop_after = _STOP
        _CACHE[key] = b.build()
    return _CACHE[key]


def _run(x, ctx, c, c_ctx, params, layers, want_ctx_out):
    nc = _program(layers, want_ctx_out)
    lay = _layer_inputs(params, list(layers))
    if _DBG_SMALL:
        for k_ in ("w_ada", "w_out", "w_ff1", "w_ff2"):
            lay[k_] = np.ascontiguousarray(lay[k_][:, :128, :128])
    ident = np.eye(128, dtype=np.float32).astype(ml_dtypes.bfloat16)
    in_maps = []
    for i in range(8):
        b, r = i // 4, i % 4
        xin = np.concatenate([x[b, r * TOWN:(r + 1) * TOWN], ctx[b]], axis=0)
        ccols = np.stack([c[b].reshape(8, 128).T, c_ctx.reshape(8, 128).T], axis=1)
        sel = np.zeros((128, 8), np.float32)
        if r > 0:
            sel[:, r - 1] = 1.0
        if r < 3:
            sel[:, 4 + r + 1] = 1.0
        m = dict(lay)
        m.update(xin=np.ascontiguousarray(xin, dtype=np.float32), ccols=np.ascontiguousarray(ccols, dtype=np.float32),
                 rope=_rope_tables(r), rcnt=_rcnt_table(r), masks=_masks(r), sel=sel, ident=ident)
        in_maps.append(m)
    res = run_bass_kernel_spmd(nc, in_maps, core_ids=list(range(8)))
    out = np.zeros((2, 4 * TOWN, D), np.float32)
    for i in range(8):
        b, r = i // 4, i % 4
        out[b, r * TOWN:(r + 1) * TOWN] = np.asarray(res.results[i]["y"])
    ctx_out = None
    if want_ctx_out:
        ctx_out = np.stack([np.asarray(res.results[0]["yc"]), np.asarray(res.results[4]["yc"])])
    return out, ctx_out


FUSED = True


def kernel(x, c, ctx, c_ctx, w_ada, b_ada, w_in, pool_w, pool_scale, swa_sink,
           mla_q_norm, mla_w_uq, mla_kv_norm, mla_w_ukv, sgu_norm_g, sgu_norm_b,
           sgu_w, sgu_b, w_out, ln1_g, ln1_b, w_ff1, w_ff2, ln2_g, ln2_b):
    params = dict(w_ada=w_ada, b_ada=b_ada, w_in=w_in, pool_w=pool_w, pool_scale=pool_scale, swa_sink=swa_sink,
                  mla_q_norm=mla_q_norm, mla_w_uq=mla_w_uq, mla_kv_norm=mla_kv_norm, mla_w_ukv=mla_w_ukv,
                  sgu_norm_g=sgu_norm_g, sgu_norm_b=sgu_norm_b, sgu_w=sgu_w, sgu_b=sgu_b, w_out=w_out,
                  ln1_g=ln1_g, ln1_b=ln1_b, w_ff1=w_ff1, w_ff2=w_ff2, ln2_g=ln2_g, ln2_b=ln2_b)
    params = {k: np.asarray(v, dtype=np.float32) for k, v in params.items()}
    x = np.asarray(x, dtype=np.float32)
    ctx = np.asarray(ctx, dtype=np.float32)
    c = np.asarray(c, dtype=np.float32)
    c_ctx = np.asarray(c_ctx, dtype=np.float32)
    if FUSED:
        out, _ = _run(x, ctx, c, c_ctx, params, [0, 1, 2, 3], False)
        return out
    for l in range(L):
        x, ctx_n = _run(x, ctx, c, c_ctx, params, [l], l < L - 1)
        if ctx_n is not None:
            ctx = ctx_n
    return x
```

```python
import numpy as np
import ml_dtypes
from contextlib import ExitStack

import concourse.bass as bass
import concourse.mybir as mybir
from concourse.bass_utils import run_bass_kernel_spmd

F32 = mybir.dt.float32
BF16 = mybir.dt.bfloat16
U8 = mybir.dt.uint8
AF = mybir.ActivationFunctionType
ALU = mybir.AluOpType

D = 1024
L = 4
TOWN = 4096
NTL = 32
NTOK = 4352
NCTX = 256
EPS = 1e-6
ALPHA = float((2 * L) ** 0.25)
SWA_SCALE = 64 ** -0.5
MLA_SCALE = 96 ** -0.5

NIN = 2496
C_POOL, C_SQ, C_SQS, C_CQ, C_U, C_SK, C_SKS, C_CKV, C_KR, C_KRS, C_TV, C_TS = (
    0, 256, 512, 768, 1024, 1280, 1536, 1792, 1920, 2016, 2112, 2240)


ENGS = ("pe", "act", "dve", "pool", "sp")


class Buf:
    __slots__ = ("name", "w", "rs", "const", "excl")

    def __init__(self, name, const=False, excl=False):
        self.name = name
        self.w = None
        self.rs = []
        self.const = const
        self.excl = excl


class Op:
    __slots__ = ("eng", "fn", "deps", "sig", "dma_key", "val", "sem", "inc")

    def __init__(self, eng, fn, dma_key=None, inc=16):
        self.eng = eng
        self.fn = fn
        self.deps = []
        self.sig = False
        self.dma_key = dma_key
        self.val = None
        self.sem = None
        self.inc = inc


class Tracker:
    def __init__(self, nc):
        self.nc = nc
        self.q = {e: [] for e in ENGS}
        self.last_dma = {}
        self.pending = {e: [] for e in ENGS}

    def _dep(self, o, p):
        if p is None or p is o:
            return
        if p.eng == o.eng and p.dma_key is None and o.dma_key is None and o.eng == "pe":
            return
        o.deps.append(p)

    mute = False

    def op(self, eng, fn, reads=(), writes=(), dma_key=None, inc=16):
        if self.mute:
            return None
        o = Op(eng, fn, dma_key, inc)
        if self.pending[eng]:
            for p in self.pending[eng]:
                if p.eng != eng or p.dma_key is not None:
                    o.deps.append(p)
            self.pending[eng] = []
        for b in reads:
            self._dep(o, b.w)
            if b.excl:
                for r in b.rs:
                    if r.eng != eng:
                        self._dep(o, r)
        for b in writes:
            self._dep(o, b.w)
            for r in b.rs:
                self._dep(o, r)
        if dma_key is not None:
            self._dep(o, self.last_dma.get(dma_key))
            self.last_dma[dma_key] = o
        for b in reads:
            if not b.const:
                b.rs.append(o)
        for b in writes:
            b.w = o
            b.rs = []
        self.q[eng].append(o)
        return o

    def barrier(self):
        lasts = []
        for e in ENGS:
            for o in reversed(self.q[e]):
                if o.dma_key is None:
                    lasts.append(o)
                    break
        lasts += list(self.last_dma.values())
        for e in ENGS:
            self.pending[e] = list(lasts)

    def finalize(self, stack):
        nc = self.nc
        for e in ENGS:
            for o in self.q[e]:
                for p in o.deps:
                    p.sig = True
        self.esem = {e: stack.enter_context(nc.semaphore("es_" + e)) for e in ENGS}
        self.dsem = {}
        self.dcnt = {}
        for e in ENGS:
            cnt = 0
            for o in self.q[e]:
                if o.dma_key is not None:
                    k = o.dma_key
                    if k not in self.dsem:
                        self.dsem[k] = stack.enter_context(nc.semaphore("ds_%d" % len(self.dsem)))
                        self.dcnt[k] = 0
                    self.dcnt[k] += o.inc
                    o.sem = self.dsem[k]
                    o.val = self.dcnt[k]
                elif o.sig:
                    cnt += 1
                    o.sem = self.esem[e]
                    o.val = cnt

    def emit(self, eng, engobj):
        seen = {}
        for o in self.q[eng]:
            for p in o.deps:
                key = id(p.sem)
                if seen.get(key, 0) >= p.val:
                    continue
                seen[key] = p.val
                engobj.wait_ge(p.sem, p.val)
            ins = o.fn(engobj)
            if o.dma_key is not None:
                if o.inc == 16:
                    ins.then_inc(o.sem, 16)
                else:
                    ins.then_inc(o.sem)
            elif o.sig:
                ins.then_inc(o.sem, 1)

    def final_waits(self, engobj):
        for k, s in self.dsem.items():
            engobj.wait_ge(s, self.dcnt[k])


class TT:
    __slots__ = ("ap", "buf")

    def __init__(self, ap, buf):
        self.ap = ap
        self.buf = buf

    def __getitem__(self, idx):
        return TT(self.ap[idx], self.buf)

    def v(self, fn):
        return TT(fn(self.ap), self.buf)


class Builder:
    ARENA = 204 * 1024

    def __init__(self, layers, want_ctx_out):
        self.layers = list(layers)
        self.want_ctx_out = want_ctx_out
        self.nc = bass.Bass("TRN2", target_bir_lowering=False)
        self.T = Tracker(self.nc)
        self.dkeys = 0
        self.uid = 0
        self.keymap = {}

    def din(self, name, shape, dt=F32):
        t = self.nc.dram_tensor(name, list(shape), dt, kind="ExternalInput")
        return TT(t.ap(), Buf(name, const=True))

    def dout(self, name, shape, dt=F32):
        t = self.nc.dram_tensor(name, list(shape), dt, kind="ExternalOutput")
        return TT(t.ap(), Buf(name))

    def dscr(self, name, shape, dt):
        t = self.nc.dram_tensor(name, list(shape), dt)
        return TT(t.ap(), Buf(name))

    def alloc(self, shape, dt, name=None):
        nb = 4 if dt == F32 else 2
        n = int(np.prod(shape[1:])) * nb
        n_al = (n + 31) // 32 * 32
        off = self.aoff
        self.aoff += n_al
        assert self.aoff <= self.ARENA, ("arena overflow", name, self.aoff)
        self.amax = max(self.amax, self.aoff)
        ap = self.arena[:, off:off + n].bitcast(dt)
        if len(shape) == 3:
            ap = ap.rearrange("p (a b) -> p a b", a=shape[1])
        elif len(shape) == 4:
            ap = ap.rearrange("p (a b c) -> p a b c", a=shape[1], b=shape[2])
        if shape[0] < 128:
            ap = ap[0:shape[0]]
        self.uid += 1
        return TT(ap, Buf("%s_%d" % (name or "t", self.uid)))

    def slots(self, n, shape, dt, name=None):
        return [self.alloc(shape, dt, name) for _ in range(n)]

    def bank(self, i, shape, dt=F32):
        ap = self.banks[i][:, :].bitcast(dt)
        nel = 512 if dt == F32 else 1024
        tot = int(np.prod(shape[1:]))
        assert tot <= nel
        ap = ap[:, 0:tot]
        if len(shape) == 3:
            ap = ap.rearrange("p (a b) -> p a b", a=shape[1])
        if shape[0] < 128:
            ap = ap[0:shape[0]]
        return TT(ap, self.bbuf[i])

    def nb(self, pool):
        i = pool[0]
        pool.append(pool.pop(0))
        return i

    def dkey(self, p="k"):
        self.dkeys += 1
        return "%s%d" % (p, self.dkeys)

    def R(self, xs):
        return [x.buf for x in xs if x is not None]

    NKEYS = 90

    def pkey(self, key, eng="sp"):
        km = self.keymap.setdefault(eng, {})
        if key not in km:
            km[key] = "%s%d" % (eng, len(km) % (60 if eng == "sp" else 25))
        return km[key]

    def dma(self, eng, out, in_, key, extra_r=(), extra_w=()):
        key = self.pkey(key, eng)
        self.T.op(eng, lambda e, o=out.ap, i=in_.ap: e.dma_start(out=o, in_=i),
                  reads=[in_.buf] + self.R(extra_r), writes=[out.buf] + self.R(extra_w), dma_key=key)

    def dma_nc(self, eng, out, in_, key):
        nc = self.nc

        def fn(e, o=out.ap, i=in_.ap):
            with nc.allow_non_contiguous_dma(reason="small strided"):
                return e.dma_start(out=o, in_=i)
        self.T.op(eng, fn, reads=[in_.buf], writes=[out.buf], dma_key=self.pkey(key, eng))

    def mm(self, out, lhsT, rhs, start, stop, sg=False):
        self.T.op("pe", lambda e, o=out.ap, l=lhsT.ap, r=rhs.ap, s=start, p=stop, sg=sg:
                  e.matmul(o, lhsT=l, rhs=r, start=s, stop=p, skip_group_check=sg),
                  reads=[lhsT.buf, rhs.buf], writes=[out.buf])

    def tr(self, out, in_):
        self.T.op("pe", lambda e, o=out.ap, i=in_.ap, d=self.ident.ap: e.transpose(out=o, in_=i, identity=d),
                  reads=[in_.buf, self.ident.buf], writes=[out.buf])

    def act(self, out, in_, func, bias=None, scale=None):
        kw = {}
        rd = [in_.buf]
        if bias is not None:
            if isinstance(bias, TT):
                kw["bias"] = bias.ap
                rd.append(bias.buf)
            else:
                kw["bias"] = bias
        if scale is not None:
            if isinstance(scale, TT):
                kw["scale"] = scale.ap
                rd.append(scale.buf)
            else:
                kw["scale"] = scale
        self.T.op("act", lambda e, o=out.ap, i=in_.ap, f=func, kw=kw: e.activation(out=o, in_=i, func=f, **kw),
                  reads=rd, writes=[out.buf])

    def tt(self, eng, out, in0, in1, op):
        self.T.op(eng, lambda e, o=out.ap, a=in0.ap, b=in1.ap, op=op: e.tensor_tensor(out=o, in0=a, in1=b, op=op),
                  reads=[in0.buf, in1.buf], writes=[out.buf])

    def ts(self, eng, out, in0, s1, op0, s2=None, op1=None):
        rd = [in0.buf]
        a1 = s1
        a2 = s2
        if isinstance(s1, TT):
            a1 = s1.ap
            rd.append(s1.buf)
        if isinstance(s2, TT):
            a2 = s2.ap
            rd.append(s2.buf)
        if op1 is None:
            fn = lambda e, o=out.ap, a=in0.ap: e.tensor_scalar(out=o, in0=a, scalar1=a1, scalar2=None, op0=op0)
        else:
            fn = lambda e, o=out.ap, a=in0.ap: e.tensor_scalar(out=o, in0=a, scalar1=a1, scalar2=a2, op0=op0, op1=op1)
        self.T.op(eng, fn, reads=rd, writes=[out.buf])

    def stt(self, out, in0, scalar, in1, op0, op1):
        rd = [in0.buf, in1.buf]
        sc = scalar
        if isinstance(scalar, TT):
            sc = scalar.ap
            rd.append(scalar.buf)
        self.T.op("dve", lambda e, o=out.ap, a=in0.ap, b=in1.ap: e.scalar_tensor_tensor(
            out=o, in0=a, scalar=sc, in1=b, op0=op0, op1=op1), reads=rd, writes=[out.buf])

    def memset(self, eng, out, val):
        self.T.op(eng, lambda e, o=out.ap, v=val: e.memset(o, v), writes=[out.buf])

    def recip(self, out, in_):
        self.T.op("dve", lambda e, o=out.ap, i=in_.ap: e.reciprocal(out=o, in_=i), reads=[in_.buf], writes=[out.buf])

    def ln_stats(self, x, nfree, work):
        stats, mv, ve, rstd, nb = work
        nchunk = (nfree + 511) // 512
        for i in range(nchunk):
            lo = i * 512
            hi = min(nfree, lo + 512)
            self.T.op("dve", lambda e, o=stats.ap[:, i, :], a=x.ap[:, lo:hi]: e.bn_stats(out=o, in_=a),
                      reads=[x.buf], writes=[stats.buf])
        self.T.op("dve", lambda e, o=mv.ap, a=stats.ap[:, 0:nchunk, :].rearrange("p a b -> p (a b)"):
                  e.bn_aggr(out=o, in_=a), reads=[stats.buf], writes=[mv.buf])
        self.ts("dve", ve, mv[:, 1:2], EPS, ALU.add)
        self.act(ve, ve, AF.Ln)
        self.act(rstd, ve, AF.Exp, scale=-0.5)
        self.ts("dve", nb, mv[:, 0:1], -1.0, ALU.mult, rstd, ALU.mult)
        return rstd, nb

    def ln_work(self):
        return (self.alloc([128, 2, 6], F32, "stats"), self.alloc([128, 2], F32, "mv"),
                self.alloc([128, 1], F32, "ve"), self.alloc([128, 1], F32, "rstd"),
                self.alloc([128, 1], F32, "nb"))

    def phase(self):
        self.T.barrier()
        self.keymap = {}
        self.aoff = self.apersist

    def build(self):
        nc = self.nc
        with ExitStack() as st:
            self.st = st
            self.arena = st.enter_context(nc.sbuf_tensor("arena", [128, self.ARENA], U8))
            self.banks = [st.enter_context(nc.psum_tensor("bank%d" % i, [128, 2048], U8)) for i in range(8)]
            self.bbuf = [Buf("bank%d" % i, excl=True) for i in range(8)]
            self.aoff = 0
            self.amax = 0
            self.declare()
            self.prologue()
            self.apersist = self.aoff
            for li, l in enumerate(self.layers):
                self.layer(li, l)
            self.T.barrier()
            self.T.finalize(st)
            T = self.T
            with nc.Block() as block:
                @block.sync
                def _(e):
                    T.emit("sp", e)

                @block.tensor
                def _(e):
                    T.emit("pe", e)

                @block.scalar
                def _(e):
                    T.emit("act", e)

                @block.vector
                def _(e):
                    T.emit("dve", e)

                @block.gpsimd
                def _(e):
                    T.emit("pool", e)
                    T.final_waits(e)
        return nc

    def declare(self):
        nl = len(self.layers)
        self.nl = nl
        d = self.din
        self.xin = d("xin", [NTOK, D])
        self.ccols = d("ccols", [128, 2, 8])
        SM = _DBG_SMALL
        self.w_ada = d("w_ada", [nl, D, 6 * D] if not SM else [nl, 128, 128])
        self.b_ada = d("b_ada", [nl, 6 * D])
        self.w_inx = d("w_inx", [nl, D, NIN])
        self.w_uqx = d("w_uqx", [nl, 256, 768])
        self.w_ukvx = d("w_ukvx", [nl, 128, 512])
        self.pool_bd = d("pool_bd", [nl, 256, 128])
        self.sgu_wT = d("sgu_wT", [nl, 512, 128])
        self.w_out = d("w_out", [nl, D, D] if not SM else [nl, 128, 128])
        self.w_ff1 = d("w_ff1", [nl, D, 4 * D] if not SM else [nl, 128, 128])
        self.w_ff2 = d("w_ff2", [nl, 4 * D, D] if not SM else [nl, 128, 128])
        self.colvec = d("colvec", [nl, 128, 8])
        self.bsT = d("bsT", [nl, 128, 256])
        self.rowvec = d("rowvec", [nl, 6, D])
        self.rope = d("rope", [4, 128, NTOK])
        self.rcnt = d("rcnt", [128, 2, NTOK])
        self.masks = d("masks", [128, 4, 128], BF16)
        self.sel = d("sel", [128, 8])
        self.ident_in = d("ident", [128, 128], BF16)
        self.y = self.dout("y", [TOWN, D])
        if self.want_ctx_out:
            self.yc = self.dout("yc", [NCTX, D])
        s = self.dscr
        self.wb_ada = s("wb_ada", [nl, D, 6 * D], BF16)
        self.wb_inx = s("wb_inx", [nl, D, NIN], BF16)
        self.wb_uqx = s("wb_uqx", [nl, 256, 768], BF16)
        self.wb_ukvx = s("wb_ukvx", [nl, 128, 512], BF16)
        self.wb_pool = s("wb_pool", [nl, 256, 128], BF16)
        self.wb_sgu = s("wb_sgu", [nl, 512, 128], BF16)
        self.wb_out = s("wb_out", [nl, D, D], BF16)
        self.wb_ff1 = s("wb_ff1", [nl, D, 4 * D], BF16)
        self.wb_ff2 = s("wb_ff2", [nl, 4 * D, D], BF16)
        self.xs = s("xs", [NTOK, D], F32)
        self.x1s = s("x1s", [NTOK, D], F32)
        self.mrow = s("mrow", [nl, 2, 6 * D], F32)
        self.zaT = s("zaT", [128, 2, 4384], F32)
        self.qsT = s("qsT", [128, 2, NTOK], BF16)
        self.ksT = s("ksT", [128, 2, 36 * 128], BF16)
        self.vs = s("vs", [36 * 128, 128], BF16)
        self.QmT = s("QmT", [96, 4, NTOK], BF16)
        self.KmT = [s("KmT%d" % j, [96, 4096], BF16) for j in range(4)]
        self.KmTc = s("KmTc", [96, 4, NCTX], BF16)
        self.Vm = [s("Vm%d" % j, [128, 4096], BF16) for j in range(4)]
        self.Vmc = s("Vmc", [NCTX, 512], BF16)
        self.ysT = s("ysT", [128, 2, NTOK], BF16)
        self.KmT_all = [s("KmT_all%d" % j, [4 * 96, 4096], BF16) for j in range(4)]
        self.Vm_all = [s("Vm_all%d" % j, [512, 4096], BF16) for j in range(4)]
        self.hk_in = s("hk_in", [128, 512], BF16)
        self.hk_all = s("hk_all", [512, 512], BF16)
        self.hv_in = s("hv_in", [256, 128], BF16)
        self.hv_all = s("hv_all", [1024, 128], BF16)
        self.hz_in = s("hz_in", [128, 32], F32)
        self.hz_all = s("hz_all", [512, 32], F32)

    def cast_layer_weights(self, li):
        ck = ["cast%d" % i for i in range(4)]
        n = [0]

        def cast(dst, src, rows, cols):
            for r0 in range(0, rows, 1024):
                r1 = min(rows, r0 + 1024)
                for c0 in range(0, cols, 1024):
                    c1 = min(cols, c0 + 1024)
                    self.dma("pool", TT(dst.ap[li, r0:r1, c0:c1], dst.buf), TT(src.ap[li, r0:r1, c0:c1], src.buf),
                             ck[n[0] % 4])
                    n[0] += 1
        if not _DBG_SMALL:
            cast(self.wb_ada, self.w_ada, D, 6 * D)
        cast(self.wb_inx, self.w_inx, D, NIN)
        cast(self.wb_uqx, self.w_uqx, 256, 768)
        cast(self.wb_ukvx, self.w_ukvx, 128, 512)
        cast(self.wb_pool, self.pool_bd, 256, 128)
        cast(self.wb_sgu, self.sgu_wT, 512, 128)
        if not _DBG_SMALL:
            cast(self.wb_out, self.w_out, D, D)
            cast(self.wb_ff1, self.w_ff1, D, 4 * D)
            cast(self.wb_ff2, self.w_ff2, 4 * D, D)

    def prologue(self):
        a = self.alloc
        self.cast_layer_weights(0)
        self.ident = a([128, 128], BF16, "ident")
        self.ident.buf.const = True
        self.dma("sp", self.ident, self.ident_in, "c_id")
        self.masks_sb = a([128, 4, 128], BF16, "masks")
        self.dma("sp", self.masks_sb, self.masks, "c_mk")
        self.sel_sb = a([128, 8], F32, "sel")
        self.dma("sp", self.sel_sb, self.sel, "c_sel")
        self.neghalf = a([128, 512], F32, "neghalf")
        self.memset("pool", self.neghalf, -0.5)
        self.ones_bf = a([128, 128], BF16, "ones_bf")
        self.memset("pool", self.ones_bf, 1.0)
        self.onesq = a([128, 128], BF16, "onesq")
        self.memset("pool", self.onesq, 1.0 / 256)
        self.oneskv = a([128, 128], BF16, "oneskv")
        self.memset("pool", self.oneskv, 1.0 / 128)
        self.ones_f = a([128, 128], F32, "ones_f")
        self.memset("pool", self.ones_f, 1.0)
        self.onesA = a([128, 128], BF16, "onesA")
        self.memset("pool", self.onesA, 0.0)
        self.memset("pool", self.onesA[:, 0:64], 1.0)
        self.onesB = a([128, 128], BF16, "onesB")
        self.memset("pool", self.onesB, 0.0)
        self.memset("pool", self.onesB[:, 64:128], 1.0)
        cc = a([128, 2, 8], F32, "cc")
        self.dma("sp", cc, self.ccols, "c_cc")
        self.s_bf = a([128, 8, 2], BF16, "s_bf")
        self.act(self.s_bf.v(lambda p: p.rearrange("p k s -> p s k")), cc, AF.Silu)
        self.mcol = a([128, 2, 48], F32, "mcol")
        zt = a([128, 2, 8], F32, "zt")
        self.memset("pool", zt, 0.0)
        self.dma("sp", self.zaT[:, :, 4112:4120], zt, "c_z1")
        self.dma("sp", self.zaT[:, :, 4376:4384], zt, "c_z2")

    def layer(self, li, l):
        last = (l == L - 1)
        xsrc = self.xin if li == 0 else self.xs
        stop = getattr(self, "stop_after", None)
        if not _DBG_SMALL:
            self.phase_M(li)
        if stop == "M":
            return
        self.phase_A(li, xsrc)
        if stop == "A":
            return
        self.phase_X(li)
        if stop == "X":
            return
        if li + 1 < self.nl:
            self.cast_layer_weights(li + 1)
        do_ctx = (not last)
        self.phase_B(li, xsrc, do_ctx)
        if stop == "B":
            return
        final = (li == self.nl - 1)
        self.phase_C(li, do_ctx, final)

    def phase_M(self, li):
        self.phase()
        a = self.alloc
        blk = self.slots(2, [128, 8, 1024], BF16, "adablk")
        bbc = a([2, 6 * D], F32, "bbc")
        self.dma("sp", bbc, self.b_ada.v(lambda p: p[li:li + 1, :].partition_broadcast(2)), "m_b")
        mr = a([2, 6 * D], F32, "mr")
        pool = [0, 1, 2, 3]
        for jb in range(6):
            w = blk[jb % 2]
            self.dma("sp", w, self.wb_ada.v(lambda p: p[li, :, jb * 1024:(jb + 1) * 1024].rearrange(
                "(k p) n -> p k n", p=128)), "m_w%d" % (jb % 2))
            for hf in range(2):
                ps = self.bank(self.nb(pool), [2, 512])
                for k in range(8):
                    self.mm(ps, self.s_bf[:, k, :], w[:, k, hf * 512:(hf + 1) * 512], k == 0, k == 7)
                c0 = jb * 1024 + hf * 512
                if jb in (1, 4):
                    self.stt(mr[:, c0:c0 + 512], ps, 1.0, bbc[:, c0:c0 + 512], ALU.add, ALU.add)
                else:
                    self.tt("dve", mr[:, c0:c0 + 512], ps, bbc[:, c0:c0 + 512], ALU.add)
        self.dma("pool", self.mrow.v(lambda p: p[li]), mr, "m_st")
        self.dma_nc("sp", self.mcol, self.mrow.v(lambda p: p[li].rearrange("s (j p) -> p s j", p=128)), "m_col")

    def mrow_bc(self, li, s, j):
        return self.mrow.v(lambda p: p[li, s:s + 1, j * D:(j + 1) * D].partition_broadcast(128))

    def ln_mod_T(self, xt, hT_dst, s, jscale, jshift, lnw, xn, tmp, tpool):
        rstd, nb = self.ln_stats(xt, D, lnw)
        self.act(xn, xt, AF.Identity, bias=nb, scale=rstd)
        pt = self.bank(self.nb(tpool), [128, 8, 128], BF16)
        for k in range(8):
            self.tr(pt[:, k, :], xn[:, k * 128:(k + 1) * 128])
        sc = self.mcol.v(lambda p: p[:, s, jscale * 8:(jscale + 1) * 8].unsqueeze(2).broadcast_to([128, 8, 128]))
        sh = self.mcol.v(lambda p: p[:, s, jshift * 8:(jshift + 1) * 8].unsqueeze(2).broadcast_to([128, 8, 128]))
        self.tt("dve", tmp, pt, sc, ALU.mult)
        self.tt("dve", hT_dst, tmp, sh, ALU.add)

    def groups(self, with_ctx=True, gsz=4, nlim=None):
        gs = []
        for t0 in range(0, NTL, gsz):
            gs.append((t0, gsz, 0))
        if nlim is not None:
            gs = gs[:nlim]
        if with_ctx:
            gs.append((NTL, 2, 1))
        return gs

    def phase_A(self, li, xsrc):
        self.phase()
        a = self.alloc
        winx = a([128, 8, NIN], BF16, "winx")
        for c in range(4):
            self.dma("sp", winx[:, 2 * c:2 * c + 2, :], self.wb_inx.v(
                lambda p: p[li, c * 256:(c + 1) * 256, :].rearrange("(k p) n -> p k n", p=128)), "a_w%d" % c)
        wuq = a([128, 2, 768], BF16, "wuq")
        self.dma("sp", wuq, self.wb_uqx.v(lambda p: p[li].rearrange("(k p) n -> p k n", p=128)), "a_wuq")
        wukv = a([128, 512], BF16, "wukv")
        self.dma("sp", wukv, self.wb_ukvx.v(lambda p: p[li]), "a_wukv")
        sguw = a([128, 4, 128], BF16, "sguw")
        self.dma("sp", sguw, self.wb_sgu.v(lambda p: p[li].rearrange("(h q) n -> q h n", q=128)), "a_sguw")
        bsT = a([128, 2, 128], F32, "bsT")
        self.dma("sp", bsT, self.bsT.v(lambda p: p[li].rearrange("p (a b) -> p a b", a=2)), "a_bsT")
        cv = a([128, 8], F32, "cv")
        self.dma("sp", cv, self.colvec.v(lambda p: p[li]), "a_cv")
        sgg = a([128, 256], F32, "sgg")
        sgb = a([128, 256], F32, "sgb")
        self.dma("sp", sgg, self.rowvec.v(lambda p: p[li, 0:1, 0:256].partition_broadcast(128)), "a_sgg")
        self.dma("sp", sgb, self.rowvec.v(lambda p: p[li, 0:1, 256:512].partition_broadcast(128)), "a_sgb")
        xts = self.slots(2, [128, D], F32, "xt")
        xns = self.slots(2, [128, D], BF16, "xn")
        tmps = self.slots(2, [128, 8, 128], F32, "tmp")
        hTs = self.slots(2, [128, 8, 512], BF16, "hT")
        ropes = self.slots(2, [128, 4, 512], F32, "rope")
        lnws = [self.ln_work() for _ in range(2)]
        lnw2 = [self.ln_work() for _ in range(2)]
        za_sb = self.slots(2, [128, 2, 512], F32, "za")
        t1s = self.slots(2, [128, 512], F32, "t1")
        t2s = self.slots(2, [128, 512], F32, "t2")
        qs_sb = self.slots(2, [128, 2, 512], BF16, "qs")
        ks_sb = self.slots(2, [128, 2, 512], BF16, "ks")
        sq = self.slots(1, [128, 2, 512], BF16, "sq")[0]
        cqg = self.slots(1, [128, 2, 512], BF16, "cqg")[0]
        veq = a([128, 512], F32, "veq")
        rstdq = a([128, 512], F32, "rstdq")
        rcq = a([128, 512], F32, "rcq")
        rsq = a([128, 512], F32, "rsq")
        Qm_sb = self.slots(2, [96, 4, 512], BF16, "Qm")
        uT = self.slots(2, [128, 2, 512], F32, "uT")
        sqk = a([128, 512], BF16, "sqk")
        ckvg = a([128, 512], BF16, "ckvg")
        vek = a([128, 512], F32, "vek")
        rstdk = a([128, 512], F32, "rstdk")
        Km_sb = self.slots(2, [96, 4, 512], BF16, "Km")
        vs_sb = self.slots(2, [128, 128], BF16, "vs")
        vg = self.slots(2, [128, 256], F32, "vg")
        vn = self.slots(2, [128, 256], F32, "vn")
        vpad = self.slots(2, [128, 4, 128], BF16, "vpad")
        for vp in vpad:
            self.memset("pool", vp, 0.0)
        tsg = self.slots(2, [128, 128], F32, "tsg")
        ys_sb = self.slots(2, [128, 2, 512], BF16, "ys")
        Vm_sb = self.slots(2, [128, 4, 128], BF16, "Vm")
        for vmt in Vm_sb:
            self.memset("pool", vmt, 0.0)
            for h in range(4):
                col = 64 if h % 2 == 0 else 0
                self.memset("dve", vmt[:, h, col:col + 1], 1.0)
        vet = self.slots(2, [128, 1], F32, "vet")
        rstdt = self.slots(2, [128, 1], F32, "rstdt")

        tpool = [0, 1]
        fpool = [2, 3, 4, 5]
        kpool = [6, 7]
        tcount = 0
        for gi, (t0, nt, s) in enumerate(self.groups(nlim=_DBG_NG)):
            ntok = nt * 128
            c0 = t0 * 128
            hT = hTs[gi % 2]
            rp = ropes[gi % 2]
            self.dma("sp", rp[:, :, 0:ntok], self.rope.v(lambda p: p[:, :, c0:c0 + ntok].rearrange("a p t -> p a t")),
                     "a_rope%d" % (gi % 2))
            for jl in range(nt):
                tg = t0 + jl
                xt = xts[tcount % 2]
                self.dma("sp", xt, xsrc[tg * 128:(tg + 1) * 128, :], "a_x%d" % (tcount % 2))
                self.ln_mod_T(xt, hT[:, :, jl * 128:(jl + 1) * 128], s, 1, 0, lnws[tcount % 2],
                              xns[tcount % 2], tmps[tcount % 2], tpool)
                tcount += 1

            def proj(coff, M):
                ps = self.bank(self.nb(fpool), [M, ntok])
                for k in range(8):
                    self.mm(ps, winx[:, k, coff:coff + M], hT[:, k, 0:ntok], k == 0, k == 7)
                return ps

            cos_s, sin_s, cos_m, sin_m = (rp[:, i, 0:ntok] for i in range(4))
            self.T.mute = not _dbgp("za")
            za = za_sb[gi % 2]
            for c in range(2):
                ps = proj(C_POOL + c * 128, 128)
                self.act(za[:, c, 0:ntok], ps, AF.Copy)
            zc0 = 8 + c0 if s == 0 else 4120
            self.dma("pool", self.zaT[:, :, zc0:zc0 + ntok], za[:, :, 0:ntok], "a_sza%d" % (gi % 2))
            self.T.mute = not _dbgp("q")
            qs = qs_sb[gi % 2]
            for hp in range(2):
                pm = proj(C_SQ + hp * 128, 128)
                t1 = t1s[hp]
                self.tt("dve", t1[:, 0:ntok], pm, cos_s, ALU.mult)
                pw = proj(C_SQS + hp * 128, 128)
                t2 = t2s[hp]
                self.tt("dve", t2[:, 0:ntok], pw, sin_s, ALU.mult)
                self.tt("dve", qs[:, hp, 0:ntok], t1[:, 0:ntok], t2[:, 0:ntok], ALU.add)
            self.dma("pool", self.qsT[:, :, c0:c0 + ntok], qs[:, :, 0:ntok], "a_sqs%d" % (gi % 2))
            self.T.mute = not _dbgp("k")
            ks = ks_sb[gi % 2]
            for gk in range(2):
                pm = proj(C_SK + gk * 128, 128)
                t1 = t1s[gk]
                self.tt("dve", t1[:, 0:ntok], pm, cos_s, ALU.mult)
                pw = proj(C_SKS + gk * 128, 128)
                t2 = t2s[gk]
                self.tt("dve", t2[:, 0:ntok], pw, sin_s, ALU.mult)
                self.tt("dve", ks[:, gk, 0:ntok], t1[:, 0:ntok], t2[:, 0:ntok], ALU.add)
            kslot = (1 + t0) * 128 if s == 0 else 34 * 128
            self.dma("pool", self.ksT[:, :, kslot:kslot + ntok], ks[:, :, 0:ntok], "a_sks%d" % (gi % 2))
            self.T.mute = not _dbgp("mq")
            for i in range(2):
                pm = proj(C_CQ + i * 128, 128)
                self.act(sq[:, i, 0:ntok], pm, AF.Square)
                self.ts("dve", cqg[:, i, 0:ntok], pm, cv[:, 2 + i:3 + i], ALU.mult)
            ps = self.bank(self.nb(fpool), [128, ntok])
            for i in range(2):
                self.mm(ps, self.onesq, sq[:, i, 0:ntok], i == 0, i == 1)
            self.ts("dve", veq[:, 0:ntok], ps, EPS, ALU.add)
            self.act(veq[:, 0:ntok], veq[:, 0:ntok], AF.Ln)
            self.act(rstdq[:, 0:ntok], veq[:, 0:ntok], AF.Exp, scale=-0.5)
            self.tt("dve", rcq[0:96, 0:ntok], rstdq[0:96, 0:ntok], cos_m[0:96], ALU.mult)
            self.tt("dve", rsq[0:96, 0:ntok], rstdq[0:96, 0:ntok], sin_m[0:96], ALU.mult)
            Qm = Qm_sb[gi % 2]
            for h in range(4):
                pq = self.bank(self.nb(fpool), [96, ntok])
                for kc in range(2):
                    self.mm(pq, wuq[:, kc, h * 96:(h + 1) * 96], cqg[:, kc, 0:ntok], kc == 0, kc == 1)
                t1 = t1s[h % 2]
                self.tt("dve", t1[0:96, 0:ntok], pq, rcq[0:96, 0:ntok], ALU.mult)
                pw = self.bank(self.nb(fpool), [96, ntok])
                for kc in range(2):
                    self.mm(pw, wuq[:, kc, 384 + h * 96:384 + (h + 1) * 96], cqg[:, kc, 0:ntok], kc == 0, kc == 1)
                t2 = t2s[h % 2]
                self.tt("dve", t2[0:96, 0:ntok], pw, rsq[0:96, 0:ntok], ALU.mult)
                self.tt("dve", Qm[:, h, 0:ntok], t1[0:96, 0:ntok], t2[0:96, 0:ntok], ALU.add)
            self.dma("pool", self.QmT[:, :, c0:c0 + ntok], Qm[:, :, 0:ntok], "a_sQm%d" % (gi % 2))
            self.T.mute = not _dbgp("u")
            u = uT[gi % 2]
            for i in range(2):
                pm = proj(C_U + i * 128, 128)
                self.act(u[:, i, 0:ntok], pm, AF.Gelu)
            self.T.mute = not _dbgp("kv")
            pm = proj(C_CKV, 128)
            self.act(sqk[:, 0:ntok], pm, AF.Square)
            self.ts("dve", ckvg[:, 0:ntok], pm, cv[:, 4:5], ALU.mult)
            ps = self.bank(self.nb(fpool), [128, ntok])
            self.mm(ps, self.oneskv, sqk[:, 0:ntok], True, True)
            self.ts("dve", vek[:, 0:ntok], ps, EPS, ALU.add)
            self.act(vek[:, 0:ntok], vek[:, 0:ntok], AF.Ln)
            self.act(rstdk[:, 0:ntok], vek[:, 0:ntok], AF.Exp, scale=-0.5)
            Km = Km_sb[gi % 2]
            for h in range(4):
                pk = self.bank(self.nb(fpool), [64, ntok])
                self.mm(pk, wukv[:, h * 64:(h + 1) * 64], ckvg[:, 0:ntok], True, True)
                self.tt("dve", Km[0:64, h, 0:ntok], pk, rstdk[0:64, 0:ntok], ALU.mult)
            pm = proj(C_KR, 96)
            t1 = t1s[0]
            self.tt("dve", t1[64:96, 0:ntok], pm[64:96], cos_m[64:96], ALU.mult)
            pw = proj(C_KRS, 96)
            t2 = t2s[0]
            self.tt("dve", t2[64:96, 0:ntok], pw[64:96], sin_m[64:96], ALU.mult)
            self.tt("dve", Km[64:96, :, 0:ntok],
                    t1.v(lambda p: p[64:96, 0:ntok].unsqueeze(1).broadcast_to([32, 4, ntok])),
                    t2.v(lambda p: p[64:96, 0:ntok].unsqueeze(1).broadcast_to([32, 4, ntok])), ALU.add)
            if s == 0:
                pj, c2 = t0 // 8, (t0 % 8) // 4
                self.dma("pool", self.KmT[pj].v(lambda p: p[:, c2 * 2048:(c2 + 1) * 2048].rearrange(
                    "d (h t) -> d h t", h=4)), Km[:, :, 0:ntok], "a_sKm%d" % (gi % 2))
            else:
                self.dma("pool", self.KmTc, Km[:, :, 0:ntok], "a_sKm%d" % (gi % 2))
            self.T.mute = not _dbgp("tok")
            ys = ys_sb[gi % 2]
            for jl in range(nt):
                tg = t0 + jl
                js = slice(jl * 128, (jl + 1) * 128)
                self.T.mute = not (_dbgp("tok") and _dbgp("tv"))
                pv = self.bank(self.nb(kpool), [128, 128])
                for k in range(8):
                    self.mm(pv, hT[:, k, js], winx[:, k, C_TV:C_TV + 128], k == 0, k == 7)
                vsb = vs_sb[jl % 2]
                self.act(vsb, pv, AF.Copy)
                vrow = (1 + tg) * 128 if s == 0 else (34 + jl) * 128
                self.dma("pool", self.vs[vrow:vrow + 128, :], vsb, "a_svs%d" % (jl % 2))
                self.T.mute = not (_dbgp("tok") and _dbgp("tg"))
                pg = self.bank(self.nb(kpool), [128, 256])
                for k in range(8):
                    self.mm(pg, hT[:, k, js], winx[:, k, C_TS:C_TS + 256], k == 0, k == 7)
                vgt = vg[jl % 2]
                self.act(vgt, pg, AF.Gelu)
                rstd, nb = self.ln_stats(vgt, 256, lnw2[jl % 2])
                vnt = vn[jl % 2]
                self.act(vnt, vgt, AF.Identity, bias=nb, scale=rstd)
                self.tt("dve", vnt, vnt, sgg, ALU.mult)
                vp = vpad[jl % 2]
                for hh in range(4):
                    co = (hh % 2) * 64
                    self.tt("dve", vp[:, hh, co:co + 64], vnt[:, hh * 64:(hh + 1) * 64], sgb[:, hh * 64:(hh + 1) * 64],
                            ALU.add)
                for pr in range(2):
                    pmx = self.bank(self.nb(kpool), [128, 128])
                    self.mm(pmx, vp[:, 2 * pr, :], sguw[:, 2 * pr, :], True, False)
                    self.mm(pmx, vp[:, 2 * pr + 1, :], sguw[:, 2 * pr + 1, :], False, True)
                    tsgt = tsg[pr]
                    self.tt("dve", tsgt, pmx, bsT[:, pr, :], ALU.add)
                    self.tt("dve", ys[:, pr, js], tsgt, u[:, pr, js], ALU.mult)
                self.T.mute = not (_dbgp("tok") and _dbgp("tm"))
                pvm = self.bank(self.nb(kpool), [128, 257])
                self.mm(pvm[:, 0:256], ckvg[:, js], wukv[:, 256:512], True, True)
                self.mm(pvm[:, 256:257], sqk[:, js], self.oneskv[:, 0:1], False, True, sg=True)
                self.ts("dve", vet[jl % 2], pvm[:, 256:257], EPS, ALU.add)
                self.act(vet[jl % 2], vet[jl % 2], AF.Ln)
                self.act(rstdt[jl % 2], vet[jl % 2], AF.Exp, scale=-0.5)
                vmt = Vm_sb[jl % 2]
                for h in range(4):
                    co = 0 if h % 2 == 0 else 64
                    self.act(vmt[:, h, co:co + 64], pvm[:, h * 64:(h + 1) * 64], AF.Identity, scale=rstdt[jl % 2])
                vm_dst = (self.Vm[tg // 8][:, (tg % 8) * 512:(tg % 8 + 1) * 512] if s == 0
                          else self.Vmc[jl * 128:(jl + 1) * 128, :])
                self.dma("pool", vm_dst, vmt.v(lambda p: p.rearrange("p h c -> p (h c)")), "a_sVm%d" % (jl % 2))
            self.T.mute = not _dbgp("tok")
            self.dma("pool", self.ysT[:, :, c0:c0 + ntok], ys[:, :, 0:ntok], "a_sys%d" % (gi % 2))
            self.T.mute = False
            if s == 0 and t0 % 8 == 4:
                pj = t0 // 8
                self.coll(self.KmT_all[pj], self.KmT[pj], "x_ck%d" % pj)
                self.coll(self.Vm_all[pj], self.Vm[pj], "x_cv%d" % pj)

    def coll(self, out, in_, key):
        rg = [[0, 1, 2, 3], [4, 5, 6, 7]]
        if _DBG_MOCKX:
            rows = in_.ap.shape[0]
            for rr in range(4):
                self.dma("pool", out[rr * rows:(rr + 1) * rows], in_, "mock%d" % rr)
            return
        self.T.op("pool", lambda e, o=out.ap, i=in_.ap: e.collective_compute(
            "AllGather", ALU.bypass, replica_groups=rg, ins=[i.opt()], outs=[o.opt()]),
            reads=[in_.buf], writes=[out.buf], dma_key="CC_" + key, inc=1)

    def phase_X(self, li):
        self.phase()
        a = self.alloc
        hk = self.hk_in.v(lambda p: p.rearrange("p (f g t) -> p f g t", f=2, g=2))
        self.dma("pool", hk[:, 0], self.ksT[:, :, 128:256], "x_h0")
        self.dma("pool", hk[:, 1], self.ksT[:, :, 32 * 128:33 * 128], "x_h1")
        self.dma("pool", self.hv_in[0:128, :], self.vs[128:256, :], "x_h2")
        self.dma("pool", self.hv_in[128:256, :], self.vs[32 * 128:33 * 128, :], "x_h3")
        hz = self.hz_in.v(lambda p: p.rearrange("p (c f t) -> p c f t", c=2, f=2))
        self.dma_nc("pool", hz[:, :, 0, :], self.zaT[:, :, 8:16], "x_h4")
        self.dma_nc("pool", hz[:, :, 1, :], self.zaT[:, :, 4096:4104], "x_h5")
        self.coll(self.hk_all, self.hk_in, "x_c2")
        self.coll(self.hv_all, self.hv_in, "x_c3")
        self.coll(self.hz_all, self.hz_in, "x_c4")
        hka = a([128, 4, 512], BF16, "hka")
        self.dma("sp", hka, self.hk_all.v(lambda p: p.rearrange("(r p) c -> p r c", p=128)), "x_l0")
        hva = a([128, 4, 2, 128], BF16, "hva")
        self.dma("sp", hva, self.hv_all.v(lambda p: p.rearrange("(r f p) c -> p r f c", r=4, f=2)), "x_l1")
        hza = a([128, 4, 32], F32, "hza")
        self.dma("sp", hza, self.hz_all.v(lambda p: p.rearrange("(r p) c -> p r c", p=128)), "x_l2")
        sel = self.sel_sb

        def pick(dst, src_fn, soff):
            for rr in range(4):
                sc = sel[:, soff + rr:soff + rr + 1]
                if rr == 0:
                    self.ts("dve", dst, src_fn(rr), sc, ALU.mult)
                else:
                    self.stt(dst, src_fn(rr), sc, dst, ALU.mult, ALU.add)
        kp = a([128, 256], BF16, "kp")
        kn = a([128, 256], BF16, "kn")
        pick(kp, lambda rr: hka[:, rr, 256:512], 0)
        pick(kn, lambda rr: hka[:, rr, 0:256], 4)
        self.dma("pool", self.ksT[:, :, 0:128], kp.v(lambda p: p.rearrange("p (g t) -> p g t", g=2)), "x_s0")
        self.dma("pool", self.ksT[:, :, 33 * 128:34 * 128], kn.v(lambda p: p.rearrange("p (g t) -> p g t", g=2)), "x_s1")
        vp = a([128, 128], BF16, "vp")
        vn = a([128, 128], BF16, "vn")
        pick(vp, lambda rr: hva[:, rr, 1, :], 0)
        pick(vn, lambda rr: hva[:, rr, 0, :], 4)
        self.dma("pool", self.vs[0:128, :], vp, "x_s2")
        self.dma("pool", self.vs[33 * 128:34 * 128, :], vn, "x_s3")
        zp = a([128, 2, 8], F32, "zp")
        zn = a([128, 2, 8], F32, "zn")
        hz4 = hza.v(lambda p: p.rearrange("p r (c f t) -> p r c f t", c=2, f=2))
        pick(zp, lambda rr: hz4[:, rr, :, 1, :], 0)
        pick(zn, lambda rr: hz4[:, rr, :, 0, :], 4)
        self.dma_nc("pool", self.zaT[:, :, 0:8], zp, "x_s4")
        self.dma_nc("pool", self.zaT[:, :, 4104:4112], zn, "x_s5")

    def phase_B(self, li, xsrc, do_ctx):
        self.phase()
        a = self.alloc
        wout = a([128, 8, D], BF16, "wout")
        self.dma("sp", wout, self.wb_out.v(lambda p: p[li].rearrange("(k p) n -> p k n", p=128)), "b_wout")
        pbd = a([128, 2, 128], BF16, "pbd")
        self.dma("sp", pbd, self.wb_pool.v(lambda p: p[li].rearrange("(c p) n -> p c n", p=128)), "b_pbd")
        cv = a([128, 8], F32, "cv")
        self.dma("sp", cv, self.colvec.v(lambda p: p[li]), "b_cv")
        esink = a([128, 2], F32, "esink")
        self.act(esink, cv[:, 5:7], AF.Exp)
        m2bc = [a([128, D], F32, "m2bc") for _ in range(2)]
        self.dma("sp", m2bc[0], self.mrow_bc(li, 0, 2), "b_m2a")
        if do_ctx:
            self.dma("sp", m2bc[1], self.mrow_bc(li, 1, 2), "b_m2b")
        g1 = a([128, D], F32, "g1")
        b1 = a([128, D], F32, "b1")
        self.dma("sp", g1, self.rowvec.v(lambda p: p[li, 2:3, :].partition_broadcast(128)), "b_g1")
        self.dma("sp", b1, self.rowvec.v(lambda p: p[li, 3:4, :].partition_broadcast(128)), "b_b1")
        ksc = a([128, 2, 256], BF16, "ksc")
        self.dma("sp", ksc, self.ksT[:, :, 34 * 128:36 * 128], "b_ksc")
        vsc = a([128, 2, 2, 2, 128], BF16, "vsc") if False else None
        vscA = a([128, 2, 2, 128], BF16, "vscA")
        vscB = a([128, 2, 2, 128], BF16, "vscB")
        self.memset("pool", vscA, 0.0)
        self.memset("pool", vscB, 0.0)
        vsrc = self.vs.v(lambda p: p[34 * 128:36 * 128, :].rearrange("(t p) (g c) -> p t g c", p=128, g=2))
        for g_ in range(2):
            self.dma_nc("sp", vscA[:, :, g_, 0:64], vsrc[:, :, g_, :], "b_vscA%d" % g_)
            self.dma_nc("sp", vscB[:, :, g_, 64:128], vsrc[:, :, g_, :], "b_vscB%d" % g_)
        Kmc = a([96, 4, 256], BF16, "Kmc")
        self.dma("sp", Kmc, self.KmTc, "b_Kmc")
        Vmc = a([128, 2, 512], BF16, "Vmc")
        self.dma("sp", Vmc, self.Vmc.v(lambda p: p.rearrange("(t p) c -> p t c", p=128)), "b_Vmc")
        zw = self.slots(2, [128, 2, 528], F32, "zw")
        rc = self.slots(2, [128, 2, 512], F32, "rc")
        S2 = a([128, 2, 528], F32, "S2")
        S4 = a([128, 2, 528], F32, "S4")
        S8 = a([128, 528], F32, "S8")
        S16 = a([128, 528], F32, "S16")
        tq = a([128, 2, 512], F32, "tq")
        dT = a([128, 2, 512], BF16, "dT")
        yT = self.slots(2, [128, 8, 512], BF16, "yT")
        qzA = self.slots(2, [128, 2, 512], BF16, "qzA")
        qzB = self.slots(2, [128, 2, 512], BF16, "qzB")
        for q_ in qzA + qzB:
            self.memset("pool", q_, 0.0)
        ksw = self.slots(2, [128, 2, 768], BF16, "ksw")
        vswA = self.slots(2, [128, 6, 2, 128], BF16, "vswA")
        vswB = self.slots(2, [128, 6, 2, 128], BF16, "vswB")
        for v_ in vswA + vswB:
            self.memset("pool", v_, 0.0)
        PT = self.slots(4, [128, 512], BF16, "PT")
        dsum = self.slots(2, [128, 256], F32, "dsum")
        QT = self.slots(2, [96, 4, 512], BF16, "QT")
        KT = self.slots(3, [96, 4, 512], BF16, "KT")
        VT = self.slots(3, [128, 4, 512], BF16, "VT")
        Osb = self.slots(2, [128, 512], F32, "Osb")
        rrow = self.slots(2, [128, 512], F32, "rrow")
        xts = self.slots(2, [128, D], F32, "xt")
        tts = self.slots(2, [128, D], F32, "tt")
        xns = self.slots(2, [128, D], F32, "xn")
        lnws = [self.ln_work() for _ in range(2)]

        opool = [0, 1, 2, 3]
        spool = [4, 5, 6, 7]
        tcount = 0
        for gi, (t0, nt, s) in enumerate(self.groups(with_ctx=do_ctx, nlim=_DBG_NGB)):
            ntok = nt * 128
            c0 = t0 * 128
            y = yT[gi % 2]
            W = ntok + 16
            zc0 = c0 if s == 0 else 4112
            z = zw[gi % 2]
            self.dma("sp", z[:, :, 0:W], self.zaT[:, :, zc0:zc0 + W], "b_zw%d" % (gi % 2))
            r = rc[gi % 2]
            self.dma("sp", r[:, :, 0:ntok], self.rcnt[:, :, c0:c0 + ntok], "b_rc%d" % (gi % 2))
            self.tt("dve", S2[:, :, 1:W], z[:, :, 0:W - 1], z[:, :, 1:W], ALU.add)
            self.tt("dve", S4[:, :, 2:W - 1], S2[:, :, 1:W - 2], S2[:, :, 3:W], ALU.add)
            self.tt("dve", S8[:, 4:W - 3], S4[:, 1, 2:W - 5], S4[:, 1, 6:W - 1], ALU.add)
            self.tt("dve", S16[64:128, 8:W - 7], S8[64:128, 4:W - 11], S8[64:128, 12:W - 3], ALU.add)
            quads = [(0, 0, S2[0:64, 0, 8:8 + ntok]), (0, 64, S4[64:128, 0, 8:8 + ntok]),
                     (1, 0, S8[0:64, 8:8 + ntok]), (1, 64, S16[64:128, 8:8 + ntok])]
            for (c, p0, Sv) in quads:
                self.tt("dve", tq[p0:p0 + 64, c, 0:ntok], Sv, r[p0:p0 + 64, c, 0:ntok], ALU.mult)
                self.tt("dve", dT[p0:p0 + 64, c, 0:ntok], tq[p0:p0 + 64, c, 0:ntok], z[p0:p0 + 64, c, 8:8 + ntok],
                        ALU.subtract)
            for c in range(2):
                ps = self.bank(self.nb(spool), [128, ntok])
                self.mm(ps, pbd[:, c, :], dT[:, c, 0:ntok], True, True)
                self.act(y[:, c, 0:ntok], ps, AF.Identity, scale=cv[:, c:c + 1])
            self.dma("sp", y[:, 6:8, 0:ntok], self.ysT[:, :, c0:c0 + ntok], "b_ys%d" % (gi % 2))
            qA = qzA[gi % 2]
            qB = qzB[gi % 2]
            self.dma("sp", qA[0:64, :, 0:ntok], self.qsT[0:64, :, c0:c0 + ntok], "b_qsA%d" % (gi % 2))
            self.dma("sp", qB[64:128, :, 0:ntok], self.qsT[64:128, :, c0:c0 + ntok], "b_qsB%d" % (gi % 2))
            if s == 0:
                kw = ksw[gi % 2]
                self.dma("sp", kw, self.ksT[:, :, c0:c0 + 768], "b_kw%d" % (gi % 2))
                vA = vswA[gi % 2]
                vB = vswB[gi % 2]
                vsrc = self.vs.v(lambda p: p[c0:c0 + 768, :].rearrange("(t p) (g c) -> p t g c", p=128, g=2))
                for g_ in range(2):
                    self.dma_nc("sp", vA[:, :, g_, 0:64], vsrc[:, :, g_, :], "b_vA%d_%d" % (gi % 2, g_))
                    self.dma_nc("sp", vB[:, :, g_, 64:128], vsrc[:, :, g_, :], "b_vB%d_%d" % (gi % 2, g_))
            for jl in range(nt):
                tg = t0 + jl
                js = slice(jl * 128, (jl + 1) * 128)
                if s == 0:
                    keys = [(kw[:, :, jl * 128:(jl + 1) * 128], vA[:, jl], vB[:, jl], 0 if tg == 0 else 1),
                            (kw[:, :, (jl + 1) * 128:(jl + 2) * 128], vA[:, jl + 1], vB[:, jl + 1], None),
                            (kw[:, :, (jl + 2) * 128:(jl + 3) * 128], vA[:, jl + 2], vB[:, jl + 2],
                             3 if tg == NTL - 1 else 2)]
                else:
                    keys = []
                keys += [(ksc[:, :, 0:128], vscA[:, 0], vscB[:, 0], None),
                         (ksc[:, :, 128:256], vscA[:, 1], vscB[:, 1], None)]
                ob = self.bank(self.nb(opool), [128, 4, 128])
                nk = len(keys)
                for ki, (kt, va, vb, mk) in enumerate(keys):
                    sb_ = self.bank(self.nb(spool), [128, 4, 128])
                    for hd in range(4):
                        gk, e = hd // 2, hd % 2
                        self.mm(sb_[:, hd, :], kt[:, gk, :], (qA if e == 0 else qB)[:, gk, js], hd == 0, hd == 3, sg=True)
                    pt = PT[(tcount * 8 + ki) % 4]
                    ptv = pt.v(lambda p: p.rearrange("p (h t) -> p h t", h=4))
                    self.act(ptv, sb_, AF.Exp, scale=SWA_SCALE)
                    if mk is not None:
                        self.tt("dve", ptv, ptv, self.masks_sb.v(
                            lambda p: p[:, mk, :].unsqueeze(1).broadcast_to([128, 4, 128])), ALU.mult)
                    for gk in range(2):
                        first = (ki == 0 and gk == 0)
                        lastk = (ki == nk - 1)
                        self.mm(ob[:, gk, :], va[:, gk, :], ptv[:, 2 * gk, :], first, False, sg=True)
                        self.mm(ob[:, gk, :], vb[:, gk, :], ptv[:, 2 * gk + 1, :], False, lastk, sg=True)
                        self.mm(ob[:, 2 + gk, :], self.onesA, ptv[:, 2 * gk, :], False, False, sg=True)
                        self.mm(ob[:, 2 + gk, :], self.onesB, ptv[:, 2 * gk + 1, :], False, lastk, sg=True)
                ds_ = dsum[tcount % 2]
                for gk in range(2):
                    self.ts("dve", ds_[:, gk * 128:(gk + 1) * 128], ob[:, 2 + gk, :], esink[:, gk:gk + 1], ALU.add)
                self.recip(ds_, ds_)
                self.tt("dve", y[:, 2:4, js], ob[:, 0:2, :], ds_.v(lambda p: p.rearrange("p (g t) -> p g t", g=2)),
                        ALU.mult)
                tcount += 1
            Q = QT[gi % 2]
            self.dma("sp", Q[:, :, 0:ntok], self.QmT[:, :, c0:c0 + ntok], "b_Q%d" % (gi % 2))
            chunks = []
            if s == 0:
                for rr in range(4):
                    for ch in range(_DBG_NCH):
                        chunks.append((rr, ch))
            chunks.append(None)
            items = []
            nload = [0]

            def load_chunk(ci):
                ck = chunks[ci]
                if ck is None:
                    return Kmc, Vmc, 2
                rr, ch = ck
                sl = nload[0] % 3
                nload[0] += 1
                K_ = KT[sl]
                V_ = VT[sl]
                pj, c2 = ch // 2, ch % 2
                self.dma("sp", K_, self.KmT_all[pj].v(lambda p: p[rr * 96:(rr + 1) * 96, c2 * 2048:(c2 + 1) * 2048].rearrange(
                    "d (h t) -> d h t", h=4)), "b_K%d" % sl)
                self.dma("sp", V_, self.Vm_all[pj].v(lambda p: p[rr * 128:(rr + 1) * 128, c2 * 2048:(c2 + 1) * 2048].rearrange(
                    "p (t c) -> p t c", t=4)), "b_V%d" % sl)
                return K_, V_, 4
            loaded = {}
            obanks = [self.bank(i, [128, ntok]) for i in range(4)]
            sp2 = [4, 5, 6, 7]
            seq = []
            for ci in range(len(chunks)):
                nkt = 2 if chunks[ci] is None else 4
                for kt in range(nkt):
                    for h in range(4):
                        seq.append((ci, kt, h))
            LA = 3
            sb_of = {}
            pt_of = {}

            def issue_S(i):
                ci, kt, h = seq[i]
                K_, V_, _ = loaded[ci]
                sbk = self.bank(self.nb(sp2), [128, ntok])
                self.mm(sbk, K_[:, h, kt * 128:(kt + 1) * 128], Q[:, h, 0:ntok], True, True)
                sb_of[i] = sbk
            for cj in range(min(3, len(chunks))):
                loaded[cj] = load_chunk(cj)
            for i in range(min(LA, len(seq))):
                issue_S(i)
            for i in range(len(seq)):
                if i + LA < len(seq):
                    issue_S(i + LA)
                ci, kt, h = seq[i]
                K_, V_, _ = loaded[ci]
                pt = PT[i % 4]
                self.act(pt[:, 0:ntok], sb_of.pop(i), AF.Exp, scale=MLA_SCALE)
                first = (i < 4)
                lastk = (i >= len(seq) - 4)
                self.mm(obanks[h], V_[:, kt, h * 128:(h + 1) * 128], pt[:, 0:ntok], first, lastk)
                if (i + 1 == len(seq) or seq[i + 1][0] != ci) and ci + 3 < len(chunks):
                    loaded[ci + 3] = load_chunk(ci + 3)
            for h in range(4):
                e = h % 2
                dp = 64 if e == 0 else 0
                rows = slice(0, 64) if e == 0 else slice(64, 128)
                osb = Osb[h % 2]
                self.act(osb[:, 0:ntok], obanks[h], AF.Copy)
                rr_ = rrow[h % 2]
                self.recip(rr_[dp:dp + 1, 0:ntok], osb[dp:dp + 1, 0:ntok])
                pb = self.bank(self.nb(sp2), [128, ntok])
                self.mm(pb, self.ones_f[dp:dp + 1, :], rr_[dp:dp + 1, 0:ntok], True, True)
                self.tt("dve", y[rows, 4 + h // 2, 0:ntok], osb[rows, 0:ntok], pb[rows], ALU.mult)
            for jl in range(nt):
                tg = t0 + jl
                js = slice(jl * 128, (jl + 1) * 128)
                xt = xts[tcount % 2]
                self.dma("sp", xt, xsrc[tg * 128:(tg + 1) * 128, :], "b_x%d" % (tcount % 2))
                t_ = tts[tcount % 2]
                for hf in range(2):
                    ps = self.bank(self.nb(sp2), [128, 512])
                    for k in range(8):
                        self.mm(ps, y[:, k, js], wout[:, k, hf * 512:(hf + 1) * 512], k == 0, k == 7)
                    self.tt("dve", t_[:, hf * 512:(hf + 1) * 512], ps, m2bc[s][:, hf * 512:(hf + 1) * 512], ALU.mult)
                self.stt(t_, xt, ALPHA, t_, ALU.mult, ALU.add)
                rstd, nb = self.ln_stats(t_, D, lnws[tcount % 2])
                xn = xns[tcount % 2]
                self.act(xn, t_, AF.Identity, bias=nb, scale=rstd)
                self.tt("dve", xn, xn, g1, ALU.mult)
                self.tt("dve", xn, xn, b1, ALU.add)
                self.dma("pool", self.x1s[tg * 128:(tg + 1) * 128, :], xn, "b_sx%d" % (tcount % 2))
                tcount += 1

    def phase_C(self, li, do_ctx, final):
        self.phase()
        a = self.alloc
        wf1 = a([128, 8, 4 * D], BF16, "wf1")
        for c in range(4):
            self.dma("sp", wf1[:, 2 * c:2 * c + 2, :], self.wb_ff1.v(
                lambda p: p[li, c * 256:(c + 1) * 256, :].rearrange("(k p) n -> p k n", p=128)), "c_w1%d" % c)
        wf2 = a([128, 32, D], BF16, "wf2")
        for c in range(4):
            self.dma("sp", wf2[:, 8 * c:8 * c + 8, :], self.wb_ff2.v(
                lambda p: p[li, c * 1024:(c + 1) * 1024, :].rearrange("(k p) n -> p k n", p=128)), "c_w2%d" % c)
        m5bc = [a([128, D], F32, "m5bc") for _ in range(2)]
        self.dma("sp", m5bc[0], self.mrow_bc(li, 0, 5), "c_m5a")
        if do_ctx:
            self.dma("sp", m5bc[1], self.mrow_bc(li, 1, 5), "c_m5b")
        g2 = a([128, D], F32, "g2")
        b2 = a([128, D], F32, "b2")
        self.dma("sp", g2, self.rowvec.v(lambda p: p[li, 4:5, :].partition_broadcast(128)), "c_g2")
        self.dma("sp", b2, self.rowvec.v(lambda p: p[li, 5:6, :].partition_broadcast(128)), "c_b2")
        x1t = self.slots(3, [128, D], F32, "x1t")
        xnb = self.slots(2, [128, D], BF16, "xnb")
        tmps = self.slots(1, [128, 8, 128], F32, "tmp")
        h2T = self.slots(1, [128, 8, 256], BF16, "h2T")[0]
        gT = self.slots(1, [128, 32, 256], BF16, "gT")[0]
        rl = self.slots(3, [128, 256], F32, "rl")
        tts = self.slots(2, [128, D], F32, "tt")
        lnws = [self.ln_work() for _ in range(2)]
        lnw3 = [self.ln_work() for _ in range(2)]
        tpool = [0]
        fpool = [1, 2, 3, 4]
        opool = [5, 6, 7]
        tcount = 0
        for gi, (t0, nt, s) in enumerate(self.groups(with_ctx=do_ctx, gsz=2, nlim=(None if _DBG_NGB is None else 2 * _DBG_NGB))):
            ntok = nt * 128
            xl = []
            for jl in range(nt):
                tg = t0 + jl
                xt = x1t[tcount % 3]
                xl.append(xt)
                self.dma("sp", xt, self.x1s[tg * 128:(tg + 1) * 128, :], "c_x%d" % (tcount % 3))
                self.ln_mod_T(xt, h2T[:, :, jl * 128:(jl + 1) * 128], s, 4, 3, lnws[tcount % 2],
                              xnb[tcount % 2], tmps[0], tpool)
                tcount += 1
            for n in range(32):
                ps = self.bank(self.nb(fpool), [128, ntok])
                for k in range(8):
                    self.mm(ps, wf1[:, k, n * 128:(n + 1) * 128], h2T[:, k, 0:ntok], k == 0, k == 7)
                r_ = rl[n % 3]
                self.act(r_[:, 0:ntok], ps, AF.Relu)
                self.tt("dve", gT[:, n, 0:ntok], r_[:, 0:ntok], ps, ALU.mult)
            for jl in range(nt):
                tg = t0 + jl
                js = slice(jl * 128, (jl + 1) * 128)
                t_ = tts[jl % 2]
                xt = xl[jl]
                for hf in range(2):
                    ps = self.bank(self.nb(opool), [128, 512])
                    for n in range(32):
                        self.mm(ps, gT[:, n, js], wf2[:, n, hf * 512:(hf + 1) * 512], n == 0, n == 31)
                    self.tt("dve", t_[:, hf * 512:(hf + 1) * 512], ps, m5bc[s][:, hf * 512:(hf + 1) * 512], ALU.mult)
                self.stt(t_, xt, ALPHA, t_, ALU.mult, ALU.add)
                rstd, nb = self.ln_stats(t_, D, lnw3[jl % 2])
                self.act(xt, t_, AF.Identity, bias=nb, scale=rstd)
                self.tt("dve", xt, xt, g2, ALU.mult)
                self.tt("dve", xt, xt, b2, ALU.add)
                if final:
                    if s == 0:
                        dst = self.y[tg * 128:(tg + 1) * 128, :]
                    else:
                        dst = self.yc[jl * 128:(jl + 1) * 128, :]
                else:
                    dst = self.xs[tg * 128:(tg + 1) * 128, :]
                self.dma("pool", dst, xt, "c_sx%d" % ((tcount - nt + jl) % 3))


def _rope_tables(r):
    t = np.arange(r * TOWN, (r + 1) * TOWN)
    row = (t // 64).astype(np.float32)
    col = (t % 64).astype(np.float32)

    def tab(d_rot, nrows, base):
        d_ax = d_rot // 2
        inv = (np.float32(10000.0) ** (-np.arange(0, d_ax, 2, dtype=np.float32) / np.float32(d_ax))).astype(np.float32)
        nf = d_ax // 2
        ar = (row[:, None] * inv[None, :]).astype(np.float32)
        ac = (col[:, None] * inv[None, :]).astype(np.float32)
        cos = np.ones((nrows, NTOK), np.float32)
        sin = np.zeros((nrows, NTOK), np.float32)
        blocks = [(ar, -1.0), (ar, 1.0), (ac, -1.0), (ac, 1.0)]
        for bi, (ang, sg) in enumerate(blocks):
            lo = base + bi * nf
            cos[lo:lo + nf, :TOWN] = np.cos(ang).T
            sin[lo:lo + nf, :TOWN] = sg * np.sin(ang).T
        return cos, sin
    cs, ss = tab(64, 128, 0)
    cs[64:128] = cs[0:64]
    ss[64:128] = ss[0:64]
    cm, sm = tab(32, 128, 64)
    return np.stack([cs, ss, cm, sm]).astype(np.float32)


def _rcnt_table(r):
    out = np.zeros((128, 2, NTOK), np.float32)
    wins = (2, 4, 8, 16)
    t = np.arange(r * TOWN, (r + 1) * TOWN)
    tc = np.arange(NCTX)
    for gi, w in enumerate(wins):
        c, p0 = gi // 2, (gi % 2) * 64
        cnt = np.clip(t + w // 2, 0, 4 * TOWN) - np.clip(t - w // 2, 0, 4 * TOWN)
        out[p0:p0 + 64, c, :TOWN] = (1.0 / cnt.astype(np.float32))[None, :]
        cntc = np.clip(tc + w // 2, 0, NCTX) - np.clip(tc - w // 2, 0, NCTX)
        out[p0:p0 + 64, c, TOWN:] = (1.0 / cntc.astype(np.float32))[None, :]
    return out


def _masks(r):
    k = np.arange(128)[:, None]
    q = np.arange(128)[None, :]
    prev = (k >= q).astype(np.float32)
    nxt = (k <= q).astype(np.float32)
    z = np.zeros_like(prev)
    m = np.stack([z if r == 0 else prev, prev, nxt, z if r == 3 else nxt], axis=1)
    return m.astype(ml_dtypes.bfloat16)


def _layer_inputs(p, ls):
    nl = len(ls)
    w_in = p["w_in"][ls]
    swq = np.array([h * 64 + j for h in range(4) for j in (list(range(16, 32)) + list(range(0, 16)) +
                                                           list(range(48, 64)) + list(range(32, 48)))])
    swk = np.array([g * 64 + j for g in range(2) for j in (list(range(16, 32)) + list(range(0, 16)) +
                                                           list(range(48, 64)) + list(range(32, 48)))])
    swr = np.array(list(range(8, 16)) + list(range(0, 8)) + list(range(24, 32)) + list(range(16, 24)))
    winx = np.zeros((nl, D, NIN), np.float32)
    winx[:, :, C_POOL:C_POOL + 256] = w_in[:, :, 0:256]
    winx[:, :, C_SQ:C_SQ + 256] = w_in[:, :, 256:512]
    winx[:, :, C_SQS:C_SQS + 256] = w_in[:, :, 256 + swq]
    winx[:, :, C_CQ:C_CQ + 256] = w_in[:, :, 512:768]
    winx[:, :, C_U:C_U + 256] = w_in[:, :, 768:1024]
    for g in range(2):
        kc = w_in[:, :, 1280 + g * 64:1280 + (g + 1) * 64]
        kcs = w_in[:, :, 1280 + swk[g * 64:(g + 1) * 64]]
        for e in range(2):
            winx[:, :, C_SK + g * 128 + e * 64:C_SK + g * 128 + (e + 1) * 64] = kc
            winx[:, :, C_SKS + g * 128 + e * 64:C_SKS + g * 128 + (e + 1) * 64] = kcs
    winx[:, :, C_CKV:C_CKV + 128] = w_in[:, :, 1536:1664]
    winx[:, :, C_KR + 64:C_KR + 96] = w_in[:, :, 1664:1696]
    winx[:, :, C_KRS + 64:C_KRS + 96] = w_in[:, :, 1664 + swr]
    winx[:, :, C_TV:C_TV + 128] = w_in[:, :, 1408:1536]
    winx[:, :, C_TS:C_TS + 256] = w_in[:, :, 1024:1280]
    wuq = p["mla_w_uq"][ls]
    wuqx = np.zeros((nl, 256, 768), np.float32)
    for h in range(4):
        wuqx[:, :, h * 96:(h + 1) * 96] = wuq[:, :, h * 96:(h + 1) * 96]
        wuqx[:, :, 384 + h * 96 + 64:384 + (h + 1) * 96] = wuq[:, :, h * 96 + 64 + swr]
    wukv = p["mla_w_ukv"][ls]
    wukvx = np.zeros((nl, 128, 512), np.float32)
    for h in range(4):
        wukvx[:, :, h * 64:(h + 1) * 64] = wukv[:, :, h * 128:h * 128 + 64]
        wukvx[:, :, 256 + h * 64:256 + (h + 1) * 64] = wukv[:, :, h * 128 + 64:(h + 1) * 128]
    pw = p["pool_w"][ls]
    pbd = np.zeros((nl, 2, 128, 128), np.float32)
    for c in range(2):
        pbd[:, c, 0:64, 0:64] = pw[:, 2 * c]
        pbd[:, c, 64:128, 64:128] = pw[:, 2 * c + 1]
    sguwT = np.ascontiguousarray(np.transpose(p["sgu_w"][ls], (0, 1, 3, 2))).reshape(nl, 512, 128)
    colvec = np.zeros((nl, 128, 8), np.float32)
    colvec[:, :, 0:2] = p["pool_scale"][ls].reshape(nl, 2, 128).transpose(0, 2, 1)
    colvec[:, :, 2:4] = p["mla_q_norm"][ls].reshape(nl, 2, 128).transpose(0, 2, 1)
    colvec[:, :, 4] = p["mla_kv_norm"][ls]
    sink = p["swa_sink"][ls]
    for gk in range(2):
        for e in range(2):
            colvec[:, e * 64:(e + 1) * 64, 5 + gk] = sink[:, 2 * gk + e][:, None]
    sb = p["sgu_b"][ls]
    bsT = np.zeros((nl, 128, 2, 128), np.float32)
    for pr in range(2):
        for e in range(2):
            bsT[:, e * 64:(e + 1) * 64, pr, :] = sb[:, 2 * pr + e][:, None, :]
    rowvec = np.zeros((nl, 6, D), np.float32)
    rowvec[:, 0, 0:256] = p["sgu_norm_g"][ls]
    rowvec[:, 0, 256:512] = p["sgu_norm_b"][ls]
    rowvec[:, 2] = p["ln1_g"][ls]
    rowvec[:, 3] = p["ln1_b"][ls]
    rowvec[:, 4] = p["ln2_g"][ls]
    rowvec[:, 5] = p["ln2_b"][ls]
    return dict(w_ada=np.ascontiguousarray(p["w_ada"][ls]), b_ada=np.ascontiguousarray(p["b_ada"][ls]),
                w_inx=winx, w_uqx=wuqx, w_ukvx=wukvx, pool_bd=pbd.reshape(nl, 256, 128), sgu_wT=sguwT,
                w_out=np.ascontiguousarray(p["w_out"][ls]), w_ff1=np.ascontiguousarray(p["w_ff1"][ls]),
                w_ff2=np.ascontiguousarray(p["w_ff2"][ls]), colvec=colvec, bsT=bsT.reshape(nl, 128, 256),
                rowvec=rowvec)


_CACHE = {}
import os as _os
_DBG_PARTS = _os.environ.get("DBG_A")
_DBG_NG = int(_os.environ["DBG_NG"]) if "DBG_NG" in _os.environ else None
_DBG_NGB = int(_os.environ["DBG_NGB"]) if "DBG_NGB" in _os.environ else None
_DBG_MOCKX = bool(int(_os.environ.get("DBG_MOCKX", "0")))
_DBG_NCH = int(_os.environ.get("DBG_NCH", "8"))
_DBG_SMALL = bool(int(_os.environ.get("DBG_SMALL", "0")))


def _dbgp(name):
    return _DBG_PARTS is None or name in _DBG_PARTS.split(",")

_STOP = None


def _program(layers, want_ctx_out):
    key = (tuple(layers), want_ctx_out)
    if key not in _CACHE:
        b = Builder(layers, want_ctx_out)
        if _STOP is not None:
            b.stop_after = _STOP
        _CACHE[key] = b.build()
    return _CACHE[key]


def _run(x, ctx, c, c_ctx, params, layers, want_ctx_out):
    nc = _program(layers, want_ctx_out)
    lay = _layer_inputs(params, list(layers))
    if _DBG_SMALL:
        for k_ in ("w_ada", "w_out", "w_ff1", "w_ff2"):
            lay[k_] = np.ascontiguousarray(lay[k_][:, :128, :128])
    ident = np.eye(128, dtype=np.float32).astype(ml_dtypes.bfloat16)
    in_maps = []
    for i in range(8):
        b, r = i // 4, i % 4
        xin = np.concatenate([x[b, r * TOWN:(r + 1) * TOWN], ctx[b]], axis=0)
        ccols = np.stack([c[b].reshape(8, 128).T, c_ctx.reshape(8, 128).T], axis=1)
        sel = np.zeros((128, 8), np.float32)
        if r > 0:
            sel[:, r - 1] = 1.0
        if r < 3:
            sel[:, 4 + r + 1] = 1.0
        m = dict(lay)
        m.update(xin=np.ascontiguousarray(xin, dtype=np.float32), ccols=np.ascontiguousarray(ccols, dtype=np.float32),
                 rope=_rope_tables(r), rcnt=_rcnt_table(r), masks=_masks(r), sel=sel, ident=ident)
        in_maps.append(m)
    res = run_bass_kernel_spmd(nc, in_maps, core_ids=list(range(8)))
    out = np.zeros((2, 4 * TOWN, D), np.float32)
    for i in range(8):
        b, r = i // 4, i % 4
        out[b, r * TOWN:(r + 1) * TOWN] = np.asarray(res.results[i]["y"])
    ctx_out = None
    if want_ctx_out:
        ctx_out = np.stack([np.asarray(res.results[0]["yc"]), np.asarray(res.results[4]["yc"])])
    return out, ctx_out


FUSED = True


def kernel(x, c, ctx, c_ctx, w_ada, b_ada, w_in, pool_w, pool_scale, swa_sink,
           mla_q_norm, mla_w_uq, mla_kv_norm, mla_w_ukv, sgu_norm_g, sgu_norm_b,
           sgu_w, sgu_b, w_out, ln1_g, ln1_b, w_ff1, w_ff2, ln2_g, ln2_b):
    params = dict(w_ada=w_ada, b_ada=b_ada, w_in=w_in, pool_w=pool_w, pool_scale=pool_scale, swa_sink=swa_sink,
                  mla_q_norm=mla_q_norm, mla_w_uq=mla_w_uq, mla_kv_norm=mla_kv_norm, mla_w_ukv=mla_w_ukv,
                  sgu_norm_g=sgu_norm_g, sgu_norm_b=sgu_norm_b, sgu_w=sgu_w, sgu_b=sgu_b, w_out=w_out,
                  ln1_g=ln1_g, ln1_b=ln1_b, w_ff1=w_ff1, w_ff2=w_ff2, ln2_g=ln2_g, ln2_b=ln2_b)
    params = {k: np.asarray(v, dtype=np.float32) for k, v in params.items()}
    x = np.asarray(x, dtype=np.float32)
    ctx = np.asarray(ctx, dtype=np.float32)
    c = np.asarray(c, dtype=np.float32)
    c_ctx = np.asarray(c_ctx, dtype=np.float32)
    if FUSED:
        out, _ = _run(x, ctx, c, c_ctx, params, [0, 1, 2, 3], False)
        return out
    for l in range(L):
        x, ctx_n = _run(x, ctx, c, c_ctx, params, [l], l < L - 1)
        if ctx_n is not None:
            ctx = ctx_n
    return x
```
